# Optimizing a Trainium2 kernel written in Bass

```python
import jax, jax.numpy as jnp
from jax import lax
import numpy as np

D_MODEL = 1024
BATCH = 4
SEQ = 4096
DEPTH = 1

CTX_LEN = 256
GRID_W = 64
ATT_HEADS = 8
ATT_KV_HEADS = 2
ATT_GROUP = ATT_HEADS // ATT_KV_HEADS
ATT_HEAD_DIM = 64
ATT_WIDTH = ATT_HEADS * ATT_HEAD_DIM
ATT_KV_WIDTH = ATT_KV_HEADS * ATT_HEAD_DIM
AXIS_DIM = ATT_HEAD_DIM // 2
ROPE_THETA = 10000.0
Q_BLOCK = 128
HG_HEADS = 4
HG_HEAD_DIM = 128
HG_WIDTH = HG_HEADS * HG_HEAD_DIM
MIX_WIDTH = ATT_WIDTH + HG_WIDTH
CHUNK = 64
D_FF = ((8 * D_MODEL // 3 + 255) // 256) * 256
EPS = 1e-6
IN_SIZES = (ATT_WIDTH, ATT_KV_WIDTH, ATT_KV_WIDTH, HG_WIDTH, HG_WIDTH, HG_WIDTH, HG_WIDTH, HG_WIDTH)
IN_COLS = sum(IN_SIZES)
IN_SPLIT_IDX = tuple(int(v) for v in np.cumsum(IN_SIZES)[:-1])

kernel_name = "hymba_gqa_hgrn2_prefix_dit_block"


def rmsnorm(x, g):
    xf = x.astype(jnp.float32)
    y = xf * lax.rsqrt(jnp.mean(xf * xf, axis=-1, keepdims=True) + EPS)
    return (y * g.astype(jnp.float32)).astype(x.dtype)


def modulate(h, shift, scale):
    return h * (1 + scale) + shift


def axial_rope_tables(n_tokens):
    n_rows = n_tokens // GRID_W
    row = jnp.repeat(jnp.arange(n_rows, dtype=jnp.float32), GRID_W)
    col = jnp.tile(jnp.arange(GRID_W, dtype=jnp.float32), n_rows)
    inv = ROPE_THETA ** (-jnp.arange(0, AXIS_DIM, 2, dtype=jnp.float32) / AXIS_DIM)
    ar = row[:, None] * inv
    ac = col[:, None] * inv
    ang = jnp.concatenate([ar, ar, ac, ac], axis=-1)
    return jnp.cos(ang), jnp.sin(ang)


def apply_rope(x, cos, sin):
    x4 = x.reshape(*x.shape[:-1], 2, 2, ATT_HEAD_DIM // 4)
    rot = jnp.stack([-x4[..., 1, :], x4[..., 0, :]], axis=-2).reshape(x.shape)
    return (x * cos[:, None, :] + rot * sin[:, None, :]).astype(x.dtype)


def latent_attention(qx, kx, vx, kc, vc):
    B, T = qx.shape[:2]
    nblk = T // Q_BLOCK
    keys = jnp.concatenate([kc, kx], axis=1)
    vals = jnp.concatenate([vc, vx], axis=1)
    qb = qx.reshape(B, nblk, Q_BLOCK, ATT_KV_HEADS, ATT_GROUP, ATT_HEAD_DIM).transpose(1, 0, 2, 3, 4, 5)
    scale = ATT_HEAD_DIM ** -0.5

    def block(q):
        s = jnp.einsum('bqkgd,bnkd->bkgqn', q, keys).astype(jnp.float32) * scale
        p = jax.nn.softmax(s, axis=-1)
        return jnp.einsum('bkgqn,bnkd->bqkgd', p.astype(vals.dtype), vals)

    o = lax.map(block, qb)
    return o.transpose(1, 0, 2, 3, 4, 5).reshape(B, T, ATT_WIDTH)


def context_attention(qc, kc, vc):
    B, L = qc.shape[:2]
    q = qc.reshape(B, L, ATT_KV_HEADS, ATT_GROUP, ATT_HEAD_DIM)
    s = jnp.einsum('bqkgd,bnkd->bkgqn', q, kc).astype(jnp.float32) * (ATT_HEAD_DIM ** -0.5)
    p = jax.nn.softmax(s, axis=-1)
    o = jnp.einsum('bkgqn,bnkd->bqkgd', p.astype(vc.dtype), vc)
    return o.reshape(B, L, ATT_WIDTH)


def gla_chunked(q, k, v, logf, s0):
    B, T, H, DK = q.shape
    n = T // CHUNK

    def to_chunks(a):
        return a.reshape(B, n, CHUNK, H, a.shape[-1]).transpose(1, 0, 3, 2, 4)

    mask = jnp.tril(jnp.ones((CHUNK, CHUNK), dtype=bool))[:, :, None]

    def step(S, inp):
        qc, kc, vc, lf = inp
        b = jnp.cumsum(lf, axis=-2)
        diff = b[:, :, :, None, :] - b[:, :, None, :, :]
        decay = jnp.where(mask, jnp.exp(jnp.minimum(diff, 0.0)), 0.0)
        A = jnp.einsum('bhtk,bhsk,bhtsk->bhts', qc, kc, decay)
        o = jnp.einsum('bhts,bhsv->bhtv', A, vc) + jnp.einsum('bhtk,bhkv->bhtv', qc * jnp.exp(b), S)
        b_last = b[:, :, -1, :]
        S_new = jnp.exp(b_last)[..., None] * S + jnp.einsum(
            'bhsk,bhsv->bhkv', kc * jnp.exp(b_last[:, :, None, :] - b), vc)
        return S_new, o

    S, o = lax.scan(step, s0, (to_chunks(q), to_chunks(k), to_chunks(v), to_chunks(logf)))
    return o.transpose(1, 0, 3, 2, 4).reshape(B, T, H, v.shape[-1]), S


def hgrn2_prep(q, i, f, lb_d):
    B, T = q.shape[:2]
    fg = lb_d + (1.0 - lb_d) * jax.nn.sigmoid(f.astype(jnp.float32))
    sh = lambda a: a.reshape(B, T, HG_HEADS, HG_HEAD_DIM)
    qh = sh(jax.nn.silu(q.astype(jnp.float32)) * (HG_HEAD_DIM ** -0.5))
    return qh, sh(1.0 - fg), sh(i.astype(jnp.float32)), sh(jnp.log(fg))


def hgrn2_bidirectional(px, pc, lb_l, gain):
    qx, ix, ffx, fbx, gx = px
    qc, ic, ffc, fbc, gc = pc
    B = qx.shape[0]
    s0 = jnp.zeros((B, HG_HEADS, HG_HEAD_DIM, HG_HEAD_DIM), jnp.float32)
    flip = lambda a: a[:, ::-1]
    oc_f, sc_f = gla_chunked(*hgrn2_prep(qc, ic, ffc, lb_l[0]), s0)
    ox_f, _ = gla_chunked(*hgrn2_prep(qx, ix, ffx, lb_l[0]), sc_f)
    oc_b, sc_b = gla_chunked(*map(flip, hgrn2_prep(qc, ic, fbc, lb_l[1])), s0)
    ox_b, _ = gla_chunked(*map(flip, hgrn2_prep(qx, ix, fbx, lb_l[1])), sc_b)

    def readout(o_f, o_b, g):
        Bn, T = g.shape[:2]
        o = rmsnorm(o_f + flip(o_b), gain).reshape(Bn, T, HG_WIDTH)
        return (o * jax.nn.silu(g.astype(jnp.float32))).astype(g.dtype)

    return readout(ox_f, ox_b, gx), readout(oc_f, oc_b, gc)


def swiglu(h, w_gu, w_down):
    a, b = jnp.split(h @ w_gu, 2, axis=-1)
    return (jax.nn.silu(a) * b) @ w_down


def setup_inputs(seed: int = 0) -> dict:
    key = jax.random.key(seed)
    ks = jax.random.split(key, 17)
    nrm = lambda k, shape, s: jax.random.normal(k, shape, jnp.float32) * s
    return {
        "x": nrm(ks[0], (BATCH, SEQ, D_MODEL), 1.0),
        "c": nrm(ks[1], (BATCH, D_MODEL), 1.0),
        "ctx": nrm(ks[2], (BATCH, CTX_LEN, D_MODEL), 1.0),
        "c_ctx": nrm(ks[3], (D_MODEL,), 1.0),
        "w_mod": nrm(ks[4], (DEPTH, D_MODEL, 6 * D_MODEL), 0.5 * D_MODEL ** -0.5),
        "b_mod": nrm(ks[5], (DEPTH, 6 * D_MODEL), 0.02),
        "g_norm1": 1.0 + nrm(ks[6], (DEPTH, D_MODEL), 0.02),
        "w_in": nrm(ks[7], (DEPTH, D_MODEL, IN_COLS), D_MODEL ** -0.5),
        "g_q": 1.0 + nrm(ks[8], (DEPTH, ATT_HEAD_DIM), 0.02),
        "g_k": 1.0 + nrm(ks[9], (DEPTH, ATT_HEAD_DIM), 0.02),
        "lb_raw": nrm(ks[10], (DEPTH + 1, 2, HG_WIDTH), 0.5),
        "g_hg": 1.0 + nrm(ks[11], (DEPTH, HG_HEAD_DIM), 0.02),
        "w_out": nrm(ks[12], (DEPTH, MIX_WIDTH, D_MODEL), MIX_WIDTH ** -0.5),
        "g_norm2": 1.0 + nrm(ks[13], (DEPTH, D_MODEL), 0.02),
        "w_gu": nrm(ks[14], (DEPTH, D_MODEL, 2 * D_FF), D_MODEL ** -0.5),
        "w_down": nrm(ks[15], (DEPTH, D_FF, D_MODEL), D_FF ** -0.5),
        "g_final": 1.0 + nrm(ks[16], (D_MODEL,), 0.02),
    }


def reference(x, c, ctx, c_ctx, w_mod, b_mod, g_norm1, w_in, g_q, g_k, lb_raw, g_hg, w_out,
              g_norm2, w_gu, w_down, g_final):
    B, T = x.shape[:2]
    cos, sin = axial_rope_tables(T)
    lb_all = jnp.cumsum(jax.nn.softmax(lb_raw.astype(jnp.float32), axis=0), axis=0)
    heads = lambda a, h, d: a.reshape(a.shape[0], a.shape[1], h, d)
    for l in range(DEPTH):
        last = l == DEPTH - 1
        mx = [m[:, None, :] for m in jnp.split(jax.nn.silu(c) @ w_mod[l] + b_mod[l], 6, axis=-1)]
        mc = jnp.split(jax.nn.silu(c_ctx) @ w_mod[l] + b_mod[l], 6, axis=-1)
        hx = modulate(rmsnorm(x, g_norm1[l]), mx[0], mx[1])
        hc = modulate(rmsnorm(ctx, g_norm1[l]), mc[0], mc[1])
        px = jnp.split(hx @ w_in[l], IN_SPLIT_IDX, axis=-1)
        pc = jnp.split(hc @ w_in[l], IN_SPLIT_IDX, axis=-1)
        qx = apply_rope(rmsnorm(heads(px[0], ATT_HEADS, ATT_HEAD_DIM), g_q[l]), cos, sin)
        kx = apply_rope(rmsnorm(heads(px[1], ATT_KV_HEADS, ATT_HEAD_DIM), g_k[l]), cos, sin)
        vx = heads(px[2], ATT_KV_HEADS, ATT_HEAD_DIM)
        kc = rmsnorm(heads(pc[1], ATT_KV_HEADS, ATT_HEAD_DIM), g_k[l])
        vc = heads(pc[2], ATT_KV_HEADS, ATT_HEAD_DIM)
        att_x = latent_attention(qx, kx, vx, kc, vc)
        hg_x, hg_c = hgrn2_bidirectional(px[3:], pc[3:], lb_all[l], g_hg[l])
        x = x + mx[2] * (jnp.concatenate([att_x, hg_x], axis=-1) @ w_out[l])
        x = x + mx[5] * swiglu(modulate(rmsnorm(x, g_norm2[l]), mx[3], mx[4]), w_gu[l], w_down[l])
        if not last:
            qc = rmsnorm(heads(pc[0], ATT_HEADS, ATT_HEAD_DIM), g_q[l])
            att_c = context_attention(qc, kc, vc)
            ctx = ctx + mc[2] * (jnp.concatenate([att_c, hg_c], axis=-1) @ w_out[l])
            ctx = ctx + mc[5] * swiglu(modulate(rmsnorm(ctx, g_norm2[l]), mc[3], mc[4]), w_gu[l], w_down[l])
    return rmsnorm(x, g_final)
```

```python
import numpy as np
from contextlib import ExitStack
import concourse.bass as bass
import concourse.mybir as mybir
from concourse.bass_utils import run_bass_kernel_spmd

F32 = mybir.dt.float32
BF16 = mybir.dt.bfloat16
AF = mybir.ActivationFunctionType
ALU = mybir.AluOpType

ENGS = ("pe", "act", "dve", "pool", "sp")
DBG = {"stop": None, "dump": False}


class Res:
    __slots__ = ("name", "w", "r", "excl")

    def __init__(self, name="", excl=False):
        self.name = name
        self.w = None
        self.r = {}
        self.excl = excl or name.startswith(("p", "z_p", "fp"))


class Prog:
    def __init__(self, nc, stack):
        self.nc = nc
        self.stack = stack
        self.thunks = {e: [] for e in ENGS}
        self.count = {e: 0 for e in ENGS}
        self.seen = {e: {} for e in ENGS}
        self.sems = {}
        for e in ENGS:
            self.sems[e] = stack.enter_context(nc.semaphore("s_" + e))
        self.dma_cnt = {}
        self.n_dsem = 0

    def dma_sem(self):
        self.n_dsem += 1
        key = "d%d" % self.n_dsem
        self.sems[key] = self.stack.enter_context(self.nc.semaphore(key))
        self.dma_cnt[key] = 0
        return key

    def _deps(self, eng, reads, writes):
        deps = {}

        def add(k, v):
            if v > deps.get(k, 0):
                deps[k] = v
        for r in reads:
            if r.w is not None:
                add(*r.w)
            if r.excl:
                for k, v in r.r.items():
                    if k != eng:
                        add(k, v)
        for w in writes:
            if w.w is not None:
                add(*w.w)
            for k, v in w.r.items():
                add(k, v)
        waits = []
        for k, v in deps.items():
            if k == eng and eng == "pe":
                continue
            if self.seen[eng].get(k, 0) >= v:
                continue
            self.seen[eng][k] = v
            waits.append((k, v))
        return waits

    def op(self, eng, fn, reads=(), writes=()):
        self.nops = getattr(self, "nops", 0) + 1
        if self.nops > DBG.get("limit", 10 ** 9):
            return 0
        waits = self._deps(eng, reads, writes)
        self.count[eng] += 1
        n = self.count[eng]
        sems = self.sems

        def thunk(e, waits=waits, fn=fn, eng=eng):
            for k, v in waits:
                e.wait_ge(sems[k], v)
            fn(e).then_inc(sems[eng], 1)
        self.thunks[eng].append(thunk)
        for r in reads:
            if r.r.get(eng, 0) < n:
                r.r[eng] = n
        for w in writes:
            w.w = (eng, n)
            w.r = {}
        return n

    def dma(self, q, key, out, in_, reads=(), writes=()):
        self.nops = getattr(self, "nops", 0) + 1
        if self.nops > DBG.get("limit", 10 ** 9):
            return 0
        waits = self._deps(q, reads, writes)
        self.dma_cnt[key] += 16
        v = self.dma_cnt[key]
        assert v < 60000
        sems = self.sems

        def thunk(e, waits=waits):
            for k, vv in waits:
                e.wait_ge(sems[k], vv)
            e.dma_start(out=out, in_=in_).then_inc(sems[key], 16)
        self.thunks[q].append(thunk)
        for r in reads:
            if r.r.get(key, 0) < v:
                r.r[key] = v
        for w in writes:
            w.w = (key, v)
            w.r = {}
        return v

    def barrier(self):
        for eng in ENGS:
            waits = []
            for k in list(self.sems.keys()):
                v = self.count[k] if k in self.count else self.dma_cnt[k]
                if k == eng or v == 0:
                    continue
                if self.seen[eng].get(k, 0) >= v:
                    continue
                self.seen[eng][k] = v
                waits.append((k, v))
            sems = self.sems

            def thunk(e, waits=waits):
                for k, vv in waits:
                    e.wait_ge(sems[k], vv)
            self.thunks[eng].append(thunk)

    def emit(self):
        nc = self.nc
        th = self.thunks
        with nc.Block() as block:
            @block.tensor
            def _(e):
                for t in th["pe"]:
                    t(e)

            @block.scalar
            def _(e):
                for t in th["act"]:
                    t(e)

            @block.vector
            def _(e):
                for t in th["dve"]:
                    t(e)

            @block.gpsimd
            def _(e):
                for t in th["pool"]:
                    t(e)

            @block.sync
            def _(e):
                for t in th["sp"]:
                    t(e)


D = 1024
NTOK = 4352
R_CTX, R_OTH, R_OWN = 0, 256, 2304
NKT = NTOK // 128
NCOL = 3328 + 128
C_ATTQ, C_ATTK, C_ATTV, C_HGQ, C_HGI, C_FA, C_FB, C_G = 0, 512, 768, 896, 1408, 1920, 2432, 2944
DFF = 2816
NFT = 22
EPS = 1e-6


def build(dump=None):
    dump = dump or {}
    nc = bass.Bass("TRN2", target_bir_lowering=False)

    def din(name, shape, dt=F32):
        return nc.dram_tensor(name, list(shape), dt, kind="ExternalInput").ap()

    xin = din("xin", [NTOK, D])
    ropec = din("ropec", [128, 4096])
    ropes = din("ropes", [128, 4096])
    cT2 = din("cT2", [128, 16])
    cTrep = din("cTrep", [128, 1024])
    wmod = din("wmod", [128, 8, 6144])
    bmodT = din("bmodT", [128, 64])
    bmodrep = din("bmodrep", [128, 2048])
    gn = din("gn", [128, 16])
    gfin = din("gfin", [128, 1024])
    gsm = din("gsm", [128, 3])
    lbraw = din("lbraw", [128, 16])
    win = din("win", [128, 8, NCOL])
    wout = din("wout", [128, 8, 1024])
    wgu = din("wgu", [NFT, 128, 8, 256])
    wdown = din("wdown", [128, NFT, 1024])
    cst = din("cst", [128, 6 * 128])
    y = nc.dram_tensor("y", [2048, D], F32, kind="ExternalOutput").ap()
    dbg_outs = {}

    with ExitStack() as gst:
        P = Prog(nc, gst)

        def rsbuf(st, name, shape, dt):
            return st.enter_context(nc.sbuf_tensor(name, list(shape), dt))

        ARENA = 170 * 1024
        arena_t = gst.enter_context(nc.sbuf_tensor("arena", [128, ARENA // 2], BF16))
        aoff = [0]

        def sbuf(st, name, shape, dt):
            elems = 1
            for d_ in shape[1:]:
                elems *= d_
            nb = elems * (4 if dt == F32 else 2)
            nb_al = (nb + 63) // 64 * 64
            o = aoff[0]
            assert o + nb_al <= ARENA, (name, o, nb_al)
            ap = arena_t[:, o // 2:(o + nb) // 2]
            if dt == F32:
                ap = ap.bitcast(F32)
            if len(shape) == 3:
                ap = ap.rearrange("p (a b) -> p a b", a=shape[1])
            elif len(shape) == 4:
                ap = ap.rearrange("p (a b c) -> p a b c", a=shape[1], b=shape[2])
            aoff[0] = o + nb_al
            return ap

        def psum(st, name, shape, dt):
            return st.enter_context(nc.psum_tensor(name, list(shape), dt))

        d_out = P.dma_sem()

        def dbg(name, ap, shape, res, dt=F32):
            if name not in dump:
                return
            t = nc.dram_tensor("dbg_" + name, list(shape), dt, kind="ExternalOutput").ap()
            P.dma("sp", d_out, t, ap, reads=res)

        cstb = rsbuf(gst, "cstb", [128, 768], BF16)
        ones32 = rsbuf(gst, "ones32", [128, 512], F32)
        modv = rsbuf(gst, "modv", [128, 6, 8], F32)
        gsm_sb = rsbuf(gst, "gsm_sb", [128, 3], F32)
        lbv = rsbuf(gst, "lbv", [128, 4, 8], F32)
        qT = rsbuf(gst, "qT", [128, 4, 2048], BF16)
        oA = rsbuf(gst, "oA", [128, 4, 2048], BF16)
        kTd = sbuf(gst, "kTd", [128, 2, NTOK], BF16)
        Vaug = sbuf(gst, "Vaug", [128, NKT + 1, 2, 130], BF16)
        Vflat = Vaug.rearrange("p a b c -> p (a b c)")
        mark1 = aoff[0]
        gscr = nc.dram_tensor("gscr", [128, 2048], F32, kind="Internal").ap()
        d_gs = P.dma_sem()
        R_gscr = Res("gscr")
        R_cst = Res("cst")
        R_modv = Res("modv")
        R_gate = Res("gate")
        R_gfin = Res("gfin")
        R_gsm = Res("gsm")
        R_lbv = Res("lbv")
        R_q = [[Res("q%d_%d" % (hd, qc)) for qc in range(4)] for hd in range(8)]
        R_k = [Res("k%d" % t) for t in range(NKT)]
        R_v = [Res("v%d" % t) for t in range(NKT)]
        R_oA = [[Res("oA%d_%d" % (hh, t)) for t in range(16)] for hh in range(4)]
        ident = cstb[:, 0:128]
        blk64 = cstb[:, 128:256]
        permT = cstb[:, 256:384]
        ones128 = cstb[:, 384:512]
        maskf = cstb[:, 512:640]
        maskb = cstb[:, 640:768]

        d_c = P.dma_sem()
        P.op("pool", lambda e: e.memset(ones32[:], 1.0), writes=[R_cst])
        P.dma("sp", P.dma_sem(), gsm_sb[:], gsm, writes=[R_gsm])

        with ExitStack() as st:
            win_bf = sbuf(st, "win_bf", [128, 8, NCOL], BF16)
            mark2 = aoff[0]
            cst32 = sbuf(st, "cst32", [128, 768], F32)
            R_cst32 = Res("cst32")
            P.dma("sp", P.dma_sem(), cst32[:], cst, writes=[R_cst32])
            P.op("dve", lambda e: e.tensor_copy(out=cstb[:], in_=cst32[:]), reads=[R_cst32], writes=[R_cst])
            NSTG = 2
            stg = [sbuf(st, "stg%d" % i, [128, 8, 512], F32) for i in range(NSTG)]
            R_stg = [Res("stg%d" % i) for i in range(NSTG)]
            d_stg = [P.dma_sem() for _ in range(NSTG)]
            stg_i = [0]

            def load_piece(src_ap, w):
                i = stg_i[0] % NSTG
                stg_i[0] += 1
                P.dma("sp", d_stg[i], stg[i][:, :, 0:w], src_ap, writes=[R_stg[i]])
                return stg[i], R_stg[i]

            NWP = (NCOL + 511) // 512
            R_win = [Res("win%d" % i) for i in range(NWP)]

            cast_engs = ["pool", "dve", "act"]
            win_done = set()

            def win_piece(pi):
                if pi >= NWP or pi in win_done:
                    return
                win_done.add(pi)
                c0 = pi * 512
                w = min(512, NCOL - c0)
                sg, rsg = load_piece(win[:, :, c0:c0 + w], w)
                eng = cast_engs[pi % 3]
                if eng == "act":
                    P.op("act", lambda e: e.copy(out=win_bf[:, :, c0:c0 + w], in_=sg[:, :, 0:w]), reads=[rsg], writes=[R_win[pi]])
                else:
                    P.op(eng, lambda e: e.tensor_copy(out=win_bf[:, :, c0:c0 + w], in_=sg[:, :, 0:w]), reads=[rsg], writes=[R_win[pi]])

            with ExitStack() as sm:
                c2 = sbuf(sm, "c2", [128, 16], F32)
                e2 = sbuf(sm, "e2", [128, 16], F32)
                sc2 = sbuf(sm, "sc2", [128, 16], F32)
                crep = sbuf(sm, "crep", [128, 1024], F32)
                erep = sbuf(sm, "erep", [128, 1024], F32)
                screp = sbuf(sm, "screp", [128, 1024], F32)
                bT = sbuf(sm, "bT", [128, 64], F32)
                brep = sbuf(sm, "brep", [128, 2048], F32)
                gn_sb = sbuf(sm, "gn_sb", [128, 16], F32)
                lbr = sbuf(sm, "lbr", [128, 16], F32)
                modfm = sbuf(sm, "modfm", [128, 64], F32)
                modrow = sbuf(sm, "modrow", [128, 4096], F32)
                R_modrow = Res("modrow")
                gate_bc = sbuf(sm, "gate_bc", [128, 2, 1024], F32)
                pmod = psum(sm, "pmod", [128, 64], F32)
                pg = psum(sm, "pg", [128, 512], F32)
                R_c2, R_crep, R_bT, R_brep, R_gn, R_lbr, R_modfm, R_pmod, R_pg = [Res(n) for n in
                    ("c2", "crep", "bT", "brep", "gn", "lbr", "modfm", "pmod", "pg")]
                R_e2, R_sc2, R_erep, R_screp = Res("e2"), Res("sc2"), Res("erep"), Res("screp")
                P.dma("sp", P.dma_sem(), c2[:], cT2, writes=[R_c2])
                P.dma("sp", P.dma_sem(), crep[:], cTrep, writes=[R_crep])
                P.dma("sp", P.dma_sem(), bT[:], bmodT, writes=[R_bT])
                P.dma("sp", P.dma_sem(), brep[:], bmodrep, writes=[R_brep])
                P.dma("sp", P.dma_sem(), gn_sb[:], gn, writes=[R_gn])
                P.dma("sp", P.dma_sem(), lbr[:], lbraw, writes=[R_lbr])
                P.op("act", lambda e: e.activation(out=e2[:], in_=c2[:], func=AF.Exp, scale=-1.0), reads=[R_c2], writes=[R_e2])
                P.op("dve", lambda e: e.tensor_scalar(out=e2[:], in0=e2[:], scalar1=1.0, scalar2=None, op0=ALU.add), reads=[R_e2], writes=[R_e2])
                P.op("dve", lambda e: e.reciprocal(out=e2[:], in_=e2[:]), reads=[R_e2], writes=[R_e2])
                P.op("dve", lambda e: e.tensor_tensor(out=sc2[:], in0=c2[:], in1=e2[:], op=ALU.mult), reads=[R_c2, R_e2], writes=[R_sc2])
                P.op("act", lambda e: e.activation(out=erep[:], in_=crep[:], func=AF.Exp, scale=-1.0), reads=[R_crep], writes=[R_erep])
                P.op("dve", lambda e: e.tensor_scalar(out=erep[:], in0=erep[:], scalar1=1.0, scalar2=None, op0=ALU.add), reads=[R_erep], writes=[R_erep])
                P.op("dve", lambda e: e.reciprocal(out=erep[:], in_=erep[:]), reads=[R_erep], writes=[R_erep])
                P.op("dve", lambda e: e.tensor_tensor(out=screp[:], in0=crep[:], in1=erep[:], op=ALU.mult), reads=[R_crep, R_erep], writes=[R_screp])
                P.op("dve", lambda e: e.tensor_tensor(out=lbv[:, 3, :], in0=lbr[:, 8:16], in1=lbr[:, 0:8], op=ALU.subtract), reads=[R_lbr], writes=[R_lbv])
                P.op("act", lambda e: e.activation(out=lbv[:, 3, :], in_=lbv[:, 3, :], func=AF.Exp), reads=[R_lbv], writes=[R_lbv])
                P.op("dve", lambda e: e.tensor_scalar(out=lbv[:, 3, :], in0=lbv[:, 3, :], scalar1=1.0, scalar2=None, op0=ALU.add), reads=[R_lbv], writes=[R_lbv])
                P.op("dve", lambda e: e.reciprocal(out=lbv[:, 0, :], in_=lbv[:, 3, :]), reads=[R_lbv], writes=[R_lbv])
                P.op("dve", lambda e: e.tensor_scalar(out=lbv[:, 1, :], in0=lbv[:, 0, :], scalar1=-1.0, scalar2=1.0, op0=ALU.mult, op1=ALU.add), reads=[R_lbv], writes=[R_lbv])
                P.op("dve", lambda e: e.tensor_scalar(out=lbv[:, 2, :], in0=lbv[:, 1, :], scalar1=-1.0, scalar2=None, op0=ALU.mult), reads=[R_lbv], writes=[R_lbv])
                for si, split in enumerate((0, 1, 3, 4)):
                    for half in range(2):
                        c0 = split * 1024 + half * 512
                        pc_ = si * 2 + half
                        sg, rsg = load_piece(wmod[:, :, c0:c0 + 512], 512)
                        for kc in range(8):
                            P.op("pe", lambda e, sg=sg, kc=kc: e.matmul(pg[0:2, :], lhsT=sc2[:, 2 * kc:2 * kc + 2], rhs=sg[:, kc, :],
                                                                        start=(kc == 0), stop=(kc == 7)), reads=[rsg, R_sc2], writes=[R_pg])
                        P.op("dve", lambda e, pc_=pc_: e.tensor_copy(out=modrow[0:2, pc_ * 512:(pc_ + 1) * 512], in_=pg[0:2, :]), reads=[R_pg], writes=[R_modrow])
                        win_piece(pc_)
                for j in range(32):
                    P.op("pe", lambda e, j=j: e.transpose(out=pmod[:, 2 * j:2 * j + 2], in_=modrow[0:2, j * 128:(j + 1) * 128], identity=cst32[0:2, 0:2]),
                         reads=[R_modrow, R_cst32], writes=[R_pmod])
                P.op("dve", lambda e: e.tensor_tensor(out=modfm[:], in0=pmod[:], in1=bT[:], op=ALU.add), reads=[R_pmod, R_bT], writes=[R_modfm])
                mf = modfm[:].rearrange("p (j v) -> p j v", v=2)
                P.op("dve", lambda e: e.scalar_tensor_tensor(out=modv[:, 0, :], in0=mf[:, 8:16, 0], scalar=1.0, in1=gn_sb[:, 0:8], op0=ALU.add, op1=ALU.mult), reads=[R_modfm, R_gn], writes=[R_modv])
                P.op("dve", lambda e: e.tensor_copy(out=modv[:, 1, :], in_=mf[:, 0:8, 0]), reads=[R_modfm], writes=[R_modv])
                P.op("dve", lambda e: e.scalar_tensor_tensor(out=modv[:, 2, :], in0=mf[:, 8:16, 1], scalar=1.0, in1=gn_sb[:, 0:8], op0=ALU.add, op1=ALU.mult), reads=[R_modfm, R_gn], writes=[R_modv])
                P.op("dve", lambda e: e.tensor_copy(out=modv[:, 3, :], in_=mf[:, 0:8, 1]), reads=[R_modfm], writes=[R_modv])
                P.op("dve", lambda e: e.scalar_tensor_tensor(out=modv[:, 4, :], in0=mf[:, 24:32, 0], scalar=1.0, in1=gn_sb[:, 8:16], op0=ALU.add, op1=ALU.mult), reads=[R_modfm, R_gn], writes=[R_modv])
                P.op("dve", lambda e: e.tensor_copy(out=modv[:, 5, :], in_=mf[:, 16:24, 0]), reads=[R_modfm], writes=[R_modv])
                for gi, split in enumerate((2, 5)):
                    for half in range(2):
                        c0 = split * 1024 + half * 512
                        sg, rsg = load_piece(wmod[:, :, c0:c0 + 512], 512)
                        for kc in range(8):
                            P.op("pe", lambda e, sg=sg, kc=kc: e.matmul(
                                pg[:], lhsT=screp[:, kc * 128:(kc + 1) * 128], rhs=sg[:, kc, :],
                                start=(kc == 0), stop=(kc == 7)), reads=[rsg, R_screp], writes=[R_pg])
                        P.op("dve", lambda e, gi=gi, half=half: e.tensor_tensor(
                            out=gate_bc[:, gi, half * 512:(half + 1) * 512], in0=pg[:],
                            in1=brep[:, gi * 1024 + half * 512: gi * 1024 + (half + 1) * 512], op=ALU.add),
                            reads=[R_pg, R_brep], writes=[R_gate])
                dbg("modv", modv[:].rearrange("p a b -> p (a b)"), [128, 48], [R_modv])
                dbg("gate", gate_bc[:].rearrange("p a b -> p (a b)"), [128, 2048], [R_gate])
                dbg("lbv", lbv[:].rearrange("p a b -> p (a b)"), [128, 32], [R_lbv])
                P.dma("sp", d_gs, gscr, gate_bc[:].rearrange("p a b -> p (a b)"), reads=[R_gate], writes=[R_gscr])

            for pi in range(NWP):
                win_piece(pi)

            def rwin(c0, ncols=128):
                return [R_win[i] for i in range(c0 // 512, (c0 + ncols - 1) // 512 + 1)]

            P.barrier()
            aoff[0] = mark2
            if DBG["stop"] == "M":
                print("nops", P.nops)
                P.emit()
                return nc

            NXB = 1
            xt = [sbuf(st, "xt%d" % i, [128, 1024], F32) for i in range(NXB)]
            R_xt = [Res("xt%d" % i) for i in range(NXB)]
            d_xt = [P.dma_sem() for _ in range(NXB)]
            xs = [sbuf(st, "xs%d" % i, [128, 1024], BF16) for i in range(NXB)]
            R_xs = [Res("xs%d" % i) for i in range(NXB)]
            stat = [sbuf(st, "stat%d" % i, [128, 4], F32) for i in range(NXB)]
            R_stat = [Res("stat%d" % i) for i in range(NXB)]
            hxT = [sbuf(st, "hxT%d" % i, [128, 8, 512], BF16) for i in range(2)]
            R_hx = [Res("hx%d" % i) for i in range(2)]
            markA = aoff[0]
            rc = sbuf(st, "rc", [128, 512], F32)
            rs_ = sbuf(st, "rs", [128, 512], F32)
            R_rope = Res("ropec")
            R_rope2 = Res("ropes")
            d_rope = P.dma_sem()
            d_rope2 = P.dma_sem()
            xsq = sbuf(st, "xsq", [128, 512], BF16)
            xg = sbuf(st, "xg", [128, 512], BF16)
            lnv = sbuf(st, "lnv", [128, 512], F32)
            t1 = sbuf(st, "t1", [128, 512], F32)
            R_xsq, R_xg, R_lnv, R_t1 = Res("xsq"), Res("xg"), Res("lnv"), Res("t1")
            endA = aoff[0]
            aoff[0] = markA
            NUB = 4
            hg_gate = [sbuf(st, "hgg%d" % i, [128, 512], F32) for i in range(4)]
            R_hgg = [Res("hgg%d" % i) for i in range(4)]
            osum = [sbuf(st, "osum%d" % i, [128, 128], F32) for i in range(NUB)]
            osq = [sbuf(st, "osq%d" % i, [128, 128], BF16) for i in range(NUB)]
            olr = [sbuf(st, "olr%d" % i, [128, 128], F32) for i in range(NUB)]
            R_osum = [Res("osum%d" % i) for i in range(NUB)]
            R_osq = [Res("osq%d" % i) for i in range(NUB)]
            R_olr = [Res("olr%d" % i) for i in range(NUB)]
            aoff[0] = max(endA, aoff[0])
            hq = [sbuf(st, "hq%d" % i, [128, 512], F32) for i in range(4)]
            he = [sbuf(st, "he%d" % i, [128, 512], F32) for i in range(1)] * 2
            hk = [sbuf(st, "hk%d" % i, [128, 512], F32) for i in range(4)]
            hB = [sbuf(st, "hB%d" % i, [128, 516], F32) for i in range(4)]
            nBm = [sbuf(st, "nBm%d" % i, [128, 4], F32) for i in range(4)]
            R_hq = [Res("hq%d" % i) for i in range(4)]
            R_he = [Res("he0")] * 4
            R_hk = [Res("hk%d" % i) for i in range(4)]
            R_hl = [Res("hl0")] * 4
            R_hB = [Res("hB%d" % i) for i in range(4)]
            R_hnB = [Res("nBm%d" % i) for i in range(4)]
            he = he * 2
            t2 = he[0]
            R_t2 = R_he[0]
            vtm = sbuf(st, "vtm", [128, 4, 512], BF16)
            R_vtm = [Res("vtm%d" % i) for i in range(4)]
            Epos = [sbuf(st, "Epos%d" % i, [128, 132], F32) for i in range(NUB)]
            Eneg = [sbuf(st, "Eneg%d" % i, [128, 132], F32) for i in range(NUB)]
            Qh = [sbuf(st, "Qh%d" % i, [128, 128], BF16) for i in range(NUB)]
            KhT = [sbuf(st, "KhT%d" % i, [128, 128], BF16) for i in range(NUB)]
            Khtm = [sbuf(st, "Khtm%d" % i, [128, 128], BF16) for i in range(NUB)]
            Abf = [sbuf(st, "Abf%d" % i, [128, 128], BF16) for i in range(NUB)]
            Smid = [sbuf(st, "Smid%d" % i, [128, 128], F32) for i in range(NUB)]
            Smidb = [sbuf(st, "Smidb%d" % i, [128, 128], BF16) for i in range(NUB)]
            R_E = [Res("E%d" % i) for i in range(NUB)]
            R_Qh = [Res("Qh%d" % i) for i in range(NUB)]
            R_KhT = [Res("KhT%d" % i) for i in range(NUB)]
            R_Khtm = [Res("Khtm%d" % i) for i in range(NUB)]
            R_Abf = [Res("Abf%d" % i) for i in range(NUB)]
            R_Smid = [Res("Smid%d" % i) for i in range(NUB)]
            R_Smidb = [Res("Smidb%d" % i) for i in range(NUB)]
            S32 = sbuf(st, "S32", [128, 8, 128], F32)
            R_S = [Res("S%d" % i) for i in range(8)]
            pT = psum(st, "pT", [128, 1024], BF16)
            pF = [psum(st, "pF%d" % i, [128, 512], F32) for i in range(2)]
            pM = psum(st, "pM", [128, 512], F32)
            pSS = psum(st, "pSS", [128, 512], F32)
            pROT = psum(st, "pROT", [128, 512], F32)
            pH = [psum(st, "pH%d" % i, [128, 512], F32) for i in range(2)]
            R_pT, R_pM, R_pSS, R_pROT = Res("pT"), Res("pM"), Res("pSS"), Res("pROT")
            R_pF = [Res("pF0"), Res("pF1")]
            R_pHA = [Res("pH0"), Res("pH1")]
            R_pHO = R_pHA
            R_pHU = R_pHA
            R_pHK = R_pHA

            P.op("pool", lambda e: e.memset(S32[:], 0.0), writes=R_S)
            for i in range(4):
                P.op("pool", lambda e, i=i: e.memset(hB[i][:, 0:1], 0.0), writes=[R_hB[i]])
            P.op("pool", lambda e: e.memset(Vaug[:, NKT, :, :], 0.0), writes=R_v)
            P.op("pool", lambda e: e.memset(Vaug[:, :, :, 0:64], 0.0), writes=R_v)
            P.op("pool", lambda e: e.memset(Vaug[:, :, :, 0:1], 1.0), writes=R_v)
            P.op("pool", lambda e: e.memset(Vaug[:, :, :, 128:129], 1.0), writes=R_v)
            P.op("pool", lambda e: e.memset(Vaug[:, :, :, 129:130], 0.0), writes=R_v)

            cnt = {"x": 0, "hx": 0, "pf": 0, "u": 0, "ph": 0}
            print("nops before sweep", P.nops)

            def build_hxT(r0, n, gi, si):
                hb = cnt["hx"] % 2
                cnt["hx"] += 1
                for j in range(n // 128):
                    xb = cnt["x"] % NXB
                    cnt["x"] += 1
                    P.dma("sp", d_xt[xb], xt[xb][:], xin[r0 + 128 * j: r0 + 128 * (j + 1), :], writes=[R_xt[xb]])
                    norm_T(xt[xb][:], R_xt[xb], xb, gi, si, hxT[hb], R_hx[hb], j * 128)
                return hxT[hb], R_hx[hb]

            def norm_T(src, rsrc, xb, gi, si, hT, rhT, col0):
                P.op("dve", lambda e: e.scalar_tensor_tensor(out=xs[xb][:], in0=src, scalar=1.0, in1=src, op0=ALU.mult, op1=ALU.mult,
                                                              accum_out=stat[xb][:, 0:1]), reads=[rsrc], writes=[R_xs[xb], R_stat[xb]])
                P.op("act", lambda e: e.activation(out=stat[xb][:, 1:2], in_=stat[xb][:, 0:1], func=AF.Ln, bias=EPS, scale=1.0 / D), reads=[R_stat[xb]], writes=[R_stat[xb]])
                P.op("act", lambda e: e.activation(out=stat[xb][:, 2:3], in_=stat[xb][:, 1:2], func=AF.Exp, scale=-0.5), reads=[R_stat[xb]], writes=[R_stat[xb]])
                P.op("act", lambda e: e.activation(out=xs[xb][:], in_=src, func=AF.Copy, scale=stat[xb][:, 2:3]), reads=[rsrc, R_stat[xb]], writes=[R_xs[xb]])
                for kc in range(8):
                    P.op("pe", lambda e, kc=kc: e.transpose(out=pT[:, kc * 128:(kc + 1) * 128], in_=xs[xb][:, kc * 128:(kc + 1) * 128], identity=ident),
                         reads=[R_xs[xb], R_cst], writes=[R_pT])
                for kc in range(8):
                    o = hT[:, kc, col0:col0 + 128]
                    i_ = pT[:, kc * 128:(kc + 1) * 128]
                    if (col0 // 128) % 2 == 0:
                        P.op("act", lambda e, o=o, i_=i_, kc=kc: e.activation(out=o, in_=i_, func=AF.Identity, bias=modv[:, si, kc:kc + 1], scale=modv[:, gi, kc:kc + 1]),
                             reads=[R_pT, R_modv], writes=[rhT])
                    else:
                        P.op("dve", lambda e, o=o, i_=i_, kc=kc: e.tensor_scalar(out=o, in0=i_, scalar1=modv[:, gi, kc:kc + 1], scalar2=modv[:, si, kc:kc + 1], op0=ALU.mult, op1=ALU.add),
                             reads=[R_pT, R_modv], writes=[rhT])

            def fm_block(c0, hT, rhT, n):
                pb = cnt["pf"] % 2
                cnt["pf"] += 1
                for kc in range(8):
                    P.op("pe", lambda e, kc=kc: e.matmul(pF[pb][:, 0:n], lhsT=win_bf[:, kc, c0:c0 + 128], rhs=hT[:, kc, 0:n], start=(kc == 0), stop=(kc == 7)),
                         reads=rwin(c0) + [rhT], writes=[R_pF[pb]])
                return pF[pb], R_pF[pb]

            def att_qk_block(c0, hT, rhT, n, gcol, rope, dst, rdst):
                X, rX = fm_block(c0, hT, rhT, n)
                P.op("act", lambda e: e.activation(out=xsq[:, 0:n], in_=X[:, 0:n], func=AF.Square), reads=[rX], writes=[R_xsq])
                P.op("dve", lambda e: e.tensor_scalar(out=xg[:, 0:n], in0=X[:, 0:n], scalar1=gsm_sb[:, gcol:gcol + 1], scalar2=None, op0=ALU.mult), reads=[rX, R_gsm], writes=[R_xg])
                P.op("pe", lambda e: e.matmul(pSS[:, 0:n], lhsT=blk64, rhs=xsq[:, 0:n], start=True, stop=True), reads=[R_xsq, R_cst], writes=[R_pSS])
                P.op("act", lambda e: e.activation(out=lnv[:, 0:n], in_=pSS[:, 0:n], func=AF.Ln, bias=EPS, scale=1.0), reads=[R_pSS], writes=[R_lnv])
                P.op("act", lambda e: e.activation(out=lnv[:, 0:n], in_=lnv[:, 0:n], func=AF.Exp, scale=-0.5), reads=[R_lnv], writes=[R_lnv])
                if rope:
                    P.op("pe", lambda e: e.matmul(pROT[:, 0:n], lhsT=permT, rhs=xg[:, 0:n], start=True, stop=True), reads=[R_xg, R_cst], writes=[R_pROT])
                    P.op("dve", lambda e: e.tensor_tensor(out=t1[:, 0:n], in0=xg[:, 0:n], in1=rc[:, 0:n], op=ALU.mult), reads=[R_xg, R_rope], writes=[R_t1])
                    P.op("dve", lambda e: e.tensor_tensor(out=t2[:, 0:n], in0=pROT[:, 0:n], in1=rs_[:, 0:n], op=ALU.mult), reads=[R_pROT, R_rope2], writes=[R_t2])
                    P.op("dve", lambda e: e.tensor_tensor(out=t1[:, 0:n], in0=t1[:, 0:n], in1=t2[:, 0:n], op=ALU.add), reads=[R_t1, R_t2], writes=[R_t1])
                    P.op("pool", lambda e: e.tensor_tensor(out=dst, in0=t1[:, 0:n], in1=lnv[:, 0:n], op=ALU.mult), reads=[R_t1, R_lnv], writes=rdst)
                else:
                    P.op("pool", lambda e: e.tensor_tensor(out=dst, in0=xg[:, 0:n], in1=lnv[:, 0:n], op=ALU.mult), reads=[R_xg, R_lnv], writes=rdst)

            def tm_block(c0, ncols, hT, rhT, j):
                for kc in range(8):
                    P.op("pe", lambda e, kc=kc: e.matmul(pM[:, 0:ncols], lhsT=hT[:, kc, j * 128:(j + 1) * 128], rhs=win_bf[:, kc, c0:c0 + ncols], start=(kc == 0), stop=(kc == 7)),
                         reads=rwin(c0, ncols) + [rhT], writes=[R_pM])

            def gate_head(d, hh, hT, rhT, n):
                b = hh
                c0 = (C_FA if d == 0 else C_FB) + hh * 128
                X, rX = fm_block(c0, hT, rhT, n)
                li = d * 4 + hh
                P.op("act", lambda e: e.activation(out=he[b][:, 0:n], in_=X[:, 0:n], func=AF.Exp, scale=-1.0), reads=[rX], writes=[R_he[b]])
                P.op("act", lambda e: e.activation(out=he[b][:, 0:n], in_=he[b][:, 0:n], func=AF.Ln, bias=1.0, scale=1.0), reads=[R_he[b]], writes=[R_he[b]])
                P.op("act", lambda e: e.activation(out=he[b][:, 0:n], in_=he[b][:, 0:n], func=AF.Exp, scale=-1.0), reads=[R_he[b]], writes=[R_he[b]])
                P.op("dve", lambda e: e.tensor_scalar(out=hk[b][:, 0:n], in0=he[b][:, 0:n], scalar1=lbv[:, 2, li:li + 1], scalar2=lbv[:, 1, li:li + 1], op0=ALU.mult, op1=ALU.add),
                     reads=[R_he[b], R_lbv], writes=[R_hk[b]])
                P.op("act", lambda e: e.activation(out=he[b][:, 0:n], in_=he[b][:, 0:n], func=AF.Ln, bias=lbv[:, 0, li:li + 1], scale=lbv[:, 1, li:li + 1]),
                     reads=[R_he[b], R_lbv], writes=[R_he[b]])
                P.op("dve", lambda e: e.tensor_tensor_scan(out=hB[b][:, 1:n + 1], data0=ones32[:, 0:n], data1=he[b][:, 0:n], initial=0.0, op0=ALU.mult, op1=ALU.add),
                     reads=[R_he[b], R_cst], writes=[R_hB[b]])
                P.op("dve", lambda e: e.tensor_scalar(out=nBm[b][:, 0:n // 128], in0=hB[b][:, 0:n].rearrange("p (t c) -> p t c", c=128)[:, :, 64], scalar1=-1.0, scalar2=None, op0=ALU.mult),
                     reads=[R_hB[b]], writes=[R_hnB[b]])

            def q_head(hh, hT, rhT, n):
                b = hh
                X, rX = fm_block(C_HGQ + hh * 128, hT, rhT, n)
                P.op("act", lambda e: e.activation(out=hq[b][:, 0:n], in_=X[:, 0:n], func=AF.Exp, scale=-1.0), reads=[rX], writes=[R_hq[b]])
                P.op("act", lambda e: e.activation(out=hq[b][:, 0:n], in_=hq[b][:, 0:n], func=AF.Ln, bias=1.0, scale=1.0), reads=[R_hq[b]], writes=[R_hq[b]])
                P.op("act", lambda e: e.activation(out=hq[b][:, 0:n], in_=hq[b][:, 0:n], func=AF.Exp, scale=-1.0), reads=[R_hq[b]], writes=[R_hq[b]])
                P.op("dve", lambda e: e.scalar_tensor_tensor(out=hq[b][:, 0:n], in0=X[:, 0:n], scalar=128.0 ** -0.5, in1=hq[b][:, 0:n], op0=ALU.mult, op1=ALU.mult),
                     reads=[rX, R_hq[b]], writes=[R_hq[b]])

            def g_head(hh, hT, rhT, n):
                b = hh
                X, rX = fm_block(C_G + hh * 128, hT, rhT, n)
                P.op("act", lambda e: e.activation(out=hg_gate[b][:, 0:n], in_=X[:, 0:n], func=AF.Exp, scale=-1.0), reads=[rX], writes=[R_hgg[b]])
                P.op("act", lambda e: e.activation(out=hg_gate[b][:, 0:n], in_=hg_gate[b][:, 0:n], func=AF.Ln, bias=1.0, scale=1.0), reads=[R_hgg[b]], writes=[R_hgg[b]])
                P.op("act", lambda e: e.activation(out=hg_gate[b][:, 0:n], in_=hg_gate[b][:, 0:n], func=AF.Exp, scale=-1.0), reads=[R_hgg[b]], writes=[R_hgg[b]])
                P.op("dve", lambda e: e.scalar_tensor_tensor(out=hg_gate[b][:, 0:n], in0=X[:, 0:n], scalar=gsm_sb[:, 2:3], in1=hg_gate[b][:, 0:n], op0=ALU.mult, op1=ALU.mult),
                     reads=[rX, R_hgg[b], R_gsm], writes=[R_hgg[b]])

            def hg_unit(d, hh, j, out_mode, tile_own):
                b = hh
                u = hh
                ph = hh
                ts = j * 128
                sidx = d * 4 + hh
                Aps = pH[ph][:, 0:128]
                Ops = pH[ph][:, 128:256]
                Ups = pH[ph][:, 256:384]
                P.op("act", lambda e: e.activation(out=Epos[u][:, 0:129], in_=hB[b][:, ts:ts + 129], func=AF.Exp, bias=nBm[b][:, j:j + 1], scale=1.0),
                     reads=[R_hB[b], R_hnB[b]], writes=[R_E[u]])
                P.op("act", lambda e: e.activation(out=Eneg[u][:, 0:129], in_=hB[b][:, ts:ts + 129], func=AF.Exp, bias=hB[b][:, ts + 64:ts + 65], scale=-1.0),
                     reads=[R_hB[b]], writes=[R_E[u]])
                yield
                if d == 0:
                    ep, en = Epos[u][:, 1:129], Eneg[u][:, 1:129]
                    dmid, dout = Eneg[u][:, 0:1], Epos[u][:, 128:129]
                else:
                    ep, en = Eneg[u][:, 0:128], Epos[u][:, 0:128]
                    dmid, dout = Epos[u][:, 128:129], Eneg[u][:, 0:1]
                P.op("pool", lambda e: e.tensor_tensor(out=KhT[u][:], in0=hk[b][:, ts:ts + 128], in1=en, op=ALU.mult), reads=[R_hk[b], R_E[u]], writes=[R_KhT[u]])
                P.op("dve", lambda e: e.tensor_scalar(out=Smid[u][:], in0=S32[:, sidx, :], scalar1=dmid, scalar2=None, op0=ALU.mult), reads=[R_S[sidx], R_E[u]], writes=[R_Smid[u]])
                if out_mode is not None:
                    P.op("pool", lambda e: e.tensor_tensor(out=Qh[u][:], in0=hq[b][:, ts:ts + 128], in1=ep, op=ALU.mult), reads=[R_hq[b], R_E[u]], writes=[R_Qh[u]])
                    P.op("act", lambda e: e.copy(out=Smidb[u][:], in_=Smid[u][:]), reads=[R_Smid[u]], writes=[R_Smidb[u]])
                yield
                P.op("pe", lambda e: e.transpose(out=pHK[ph], in_=KhT[u][:], identity=ident), reads=[R_KhT[u], R_cst], writes=[R_pHK[ph]])
                if out_mode is not None:
                    if d == 0:
                        P.op("pe", lambda e: e.matmul(Aps[0:64, 0:128], lhsT=KhT[u][:, 0:64], rhs=Qh[u][:, 0:128], start=True, stop=True), reads=[R_KhT[u], R_Qh[u]], writes=[R_pHA[ph]])
                        P.op("pe", lambda e: e.matmul(Aps[64:128, 64:128], lhsT=KhT[u][:, 64:128], rhs=Qh[u][:, 64:128], start=True, stop=True), reads=[R_KhT[u], R_Qh[u]], writes=[R_pHA[ph]])
                    else:
                        P.op("pe", lambda e: e.matmul(Aps[64:128, 0:128], lhsT=KhT[u][:, 64:128], rhs=Qh[u][:, 0:128], start=True, stop=True), reads=[R_KhT[u], R_Qh[u]], writes=[R_pHA[ph]])
                        P.op("pe", lambda e: e.matmul(Aps[0:64, 0:64], lhsT=KhT[u][:, 0:64], rhs=Qh[u][:, 0:64], start=True, stop=True), reads=[R_KhT[u], R_Qh[u]], writes=[R_pHA[ph]])
                yield
                P.op("act", lambda e: e.copy(out=Khtm[u][:], in_=pHK[ph]), reads=[R_pHK[ph]], writes=[R_Khtm[u]])
                if out_mode is not None:
                    if d == 0:
                        P.op("dve", lambda e: e.tensor_tensor(out=Abf[u][0:64, 0:128], in0=Aps[0:64, 0:128], in1=maskf[0:64, 0:128], op=ALU.mult), reads=[R_pHA[ph], R_cst], writes=[R_Abf[u]])
                        P.op("dve", lambda e: e.tensor_tensor(out=Abf[u][64:128, 64:128], in0=Aps[64:128, 64:128], in1=maskf[64:128, 64:128], op=ALU.mult), reads=[R_pHA[ph], R_cst], writes=[R_Abf[u]])
                    else:
                        P.op("dve", lambda e: e.tensor_tensor(out=Abf[u][64:128, 0:128], in0=Aps[64:128, 0:128], in1=maskb[64:128, 0:128], op=ALU.mult), reads=[R_pHA[ph], R_cst], writes=[R_Abf[u]])
                        P.op("dve", lambda e: e.tensor_tensor(out=Abf[u][0:64, 0:64], in0=Aps[0:64, 0:64], in1=maskb[0:64, 0:64], op=ALU.mult), reads=[R_pHA[ph], R_cst], writes=[R_Abf[u]])
                yield
                if out_mode is not None:
                    P.op("pe", lambda e: e.matmul(Ops, lhsT=Smidb[u][:], rhs=Qh[u][:], start=True, stop=False), reads=[R_Smidb[u], R_Qh[u]], writes=[R_pHO[ph]])
                    P.op("pe", lambda e: e.matmul(Ops, lhsT=vtm[:, j, hh * 128:(hh + 1) * 128], rhs=Abf[u][:], start=False, stop=True), reads=[R_vtm[j], R_Abf[u]], writes=[R_pHO[ph]])
                P.op("pe", lambda e: e.matmul(Ups, lhsT=Khtm[u][:], rhs=vtm[:, j, hh * 128:(hh + 1) * 128], start=True, stop=True), reads=[R_Khtm[u], R_vtm[j]], writes=[R_pHU[ph]])
                yield
                P.op("dve", lambda e: e.tensor_tensor(out=Smid[u][:], in0=Ups, in1=Smid[u][:], op=ALU.add), reads=[R_pHU[ph], R_Smid[u]], writes=[R_Smid[u]])
                if out_mode == "store":
                    P.op("act", lambda e: e.copy(out=oA[:, hh, tile_own * 128:(tile_own + 1) * 128], in_=Ops), reads=[R_pHO[ph]], writes=[R_oA[hh][tile_own]])
                elif out_mode == "final":
                    osl = oA[:, hh, tile_own * 128:(tile_own + 1) * 128]
                    P.op("dve", lambda e: e.tensor_tensor(out=osum[u][:], in0=Ops, in1=osl, op=ALU.add), reads=[R_pHO[ph], R_oA[hh][tile_own]], writes=[R_osum[u]])
                yield
                P.op("dve", lambda e: e.tensor_scalar(out=S32[:, sidx, :], in0=Smid[u][:], scalar1=dout, scalar2=None, op0=ALU.mult), reads=[R_Smid[u], R_E[u]], writes=[R_S[sidx]])
                if out_mode == "final":
                    P.op("pool", lambda e: e.tensor_tensor(out=osq[u][:], in0=osum[u][:], in1=osum[u][:], op=ALU.mult), reads=[R_osum[u]], writes=[R_osq[u]])
                    yield
                    P.op("pe", lambda e: e.matmul(Aps, lhsT=ones128, rhs=osq[u][:], start=True, stop=True), reads=[R_osq[u], R_cst], writes=[R_pHA[ph]])
                    yield
                    P.op("act", lambda e: e.activation(out=olr[u][:], in_=Aps, func=AF.Ln, bias=EPS, scale=1.0), reads=[R_pHA[ph]], writes=[R_olr[u]])
                    yield
                    P.op("act", lambda e: e.activation(out=olr[u][:], in_=olr[u][:], func=AF.Exp, scale=-0.5), reads=[R_olr[u]], writes=[R_olr[u]])
                    yield
                    P.op("pool", lambda e: e.tensor_tensor(out=osum[u][:], in0=osum[u][:], in1=olr[u][:], op=ALU.mult), reads=[R_osum[u], R_olr[u]], writes=[R_osum[u]])
                    yield
                    P.op("pool", lambda e: e.tensor_tensor(out=osl, in0=osum[u][:], in1=hg_gate[b][:, ts:ts + 128], op=ALU.mult), reads=[R_osum[u], R_hgg[b]], writes=[R_oA[hh][tile_own]])

            def lockstep(gens):
                gens = list(gens)
                while gens:
                    nxt = []
                    for g in gens:
                        try:
                            next(g)
                            nxt.append(g)
                        except StopIteration:
                            pass
                    gens = nxt

            pH = pH + [pSS, pROT]
            R_pHA += [R_pSS, R_pROT]
            pHK = [pH[i][:, 384:512].bitcast(BF16)[:, 0:128] for i in range(4)]


            def chain(gens):
                for g in gens:
                    yield from g

            def build_tile_gen(r0, j, gi, si, hb):
                xb = 0
                hT, rhT = hxT[hb], R_hx[hb]
                col0 = j * 128
                src, rsrc = xt[xb][:], R_xt[xb]
                P.dma("sp", d_xt[xb], xt[xb][:], xin[r0 + 128 * j: r0 + 128 * (j + 1), :], writes=[R_xt[xb]])
                P.op("dve", lambda e: e.scalar_tensor_tensor(out=xs[xb][:], in0=src, scalar=1.0, in1=src, op0=ALU.mult, op1=ALU.mult,
                                                              accum_out=stat[xb][:, 0:1]), reads=[rsrc], writes=[R_xs[xb], R_stat[xb]])
                yield
                P.op("act", lambda e: e.activation(out=stat[xb][:, 1:2], in_=stat[xb][:, 0:1], func=AF.Ln, bias=EPS, scale=1.0 / D), reads=[R_stat[xb]], writes=[R_stat[xb]])
                yield
                P.op("act", lambda e: e.activation(out=stat[xb][:, 2:3], in_=stat[xb][:, 1:2], func=AF.Exp, scale=-0.5), reads=[R_stat[xb]], writes=[R_stat[xb]])
                yield
                P.op("act", lambda e: e.activation(out=xs[xb][:], in_=src, func=AF.Copy, scale=stat[xb][:, 2:3]), reads=[rsrc, R_stat[xb]], writes=[R_xs[xb]])
                yield
                for kc in range(8):
                    P.op("pe", lambda e, kc=kc: e.transpose(out=pT[:, kc * 128:(kc + 1) * 128], in_=xs[xb][:, kc * 128:(kc + 1) * 128], identity=ident),
                         reads=[R_xs[xb], R_cst], writes=[R_pT])
                yield
                for kc in range(8):
                    o = hT[:, kc, col0:col0 + 128]
                    i_ = pT[:, kc * 128:(kc + 1) * 128]
                    if j % 2 == 0:
                        P.op("act", lambda e, o=o, i_=i_, kc=kc: e.activation(out=o, in_=i_, func=AF.Identity, bias=modv[:, si, kc:kc + 1], scale=modv[:, gi, kc:kc + 1]),
                             reads=[R_pT, R_modv], writes=[rhT])
                    else:
                        P.op("dve", lambda e, o=o, i_=i_, kc=kc: e.tensor_scalar(out=o, in0=i_, scalar1=modv[:, gi, kc:kc + 1], scalar2=modv[:, si, kc:kc + 1], op0=ALU.mult, op1=ALU.add),
                             reads=[R_pT, R_modv], writes=[rhT])

            def att_qk_gen(c0, hT, rhT, n, gcol, rope, dst, rdst):
                X, rX = fm_block(c0, hT, rhT, n)
                ob_ = 1 - ((cnt["pf"] - 1) % 2)
                pSSl, R_pSSl = pF[ob_], R_pF[ob_]
                pROTl, R_pROTl = pM, R_pM
                yield
                P.op("act", lambda e: e.activation(out=xsq[:, 0:n], in_=X[:, 0:n], func=AF.Square), reads=[rX], writes=[R_xsq])
                P.op("dve", lambda e: e.tensor_scalar(out=xg[:, 0:n], in0=X[:, 0:n], scalar1=gsm_sb[:, gcol:gcol + 1], scalar2=None, op0=ALU.mult), reads=[rX, R_gsm], writes=[R_xg])
                yield
                P.op("pe", lambda e: e.matmul(pSSl[:, 0:n], lhsT=blk64, rhs=xsq[:, 0:n], start=True, stop=True), reads=[R_xsq, R_cst], writes=[R_pSSl])
                yield
                P.op("act", lambda e: e.activation(out=lnv[:, 0:n], in_=pSSl[:, 0:n], func=AF.Ln, bias=EPS, scale=1.0), reads=[R_pSSl], writes=[R_lnv])
                if rope:
                    P.op("pe", lambda e: e.matmul(pROTl[:, 0:n], lhsT=permT, rhs=xg[:, 0:n], start=True, stop=True), reads=[R_xg, R_cst], writes=[R_pROTl])
                    P.op("dve", lambda e: e.tensor_tensor(out=t1[:, 0:n], in0=xg[:, 0:n], in1=rc[:, 0:n], op=ALU.mult), reads=[R_xg, R_rope], writes=[R_t1])
                yield
                P.op("act", lambda e: e.activation(out=lnv[:, 0:n], in_=lnv[:, 0:n], func=AF.Exp, scale=-0.5), reads=[R_lnv], writes=[R_lnv])
                if rope:
                    P.op("dve", lambda e: e.tensor_tensor(out=t2[:, 0:n], in0=pROTl[:, 0:n], in1=rs_[:, 0:n], op=ALU.mult), reads=[R_pROTl, R_rope2], writes=[R_t2])
                    yield
                    P.op("dve", lambda e: e.tensor_tensor(out=t1[:, 0:n], in0=t1[:, 0:n], in1=t2[:, 0:n], op=ALU.add), reads=[R_t1, R_t2], writes=[R_t1])
                    yield
                    P.op("pool", lambda e: e.tensor_tensor(out=dst, in0=t1[:, 0:n], in1=lnv[:, 0:n], op=ALU.mult), reads=[R_t1, R_lnv], writes=rdst)
                else:
                    yield
                    P.op("pool", lambda e: e.tensor_tensor(out=dst, in0=xg[:, 0:n], in1=lnv[:, 0:n], op=ALU.mult), reads=[R_xg, R_lnv], writes=rdst)

            def v_gen(hT, rhT, j, kt):
                tm_block(C_ATTV, 128, hT, rhT, j)
                yield
                P.op("act", lambda e: e.copy(out=Vaug[:, kt, :, 64:128], in_=pM[:, 0:128].rearrange("p (k d) -> p k d", k=2)), reads=[R_pM], writes=[R_v[kt]])

            passes = [("A", "ctx", R_CTX, 256)] + [("A", "oth", R_OTH + 512 * i, 512) for i in range(4)] + \
                     [("A", "own", R_OWN + 512 * i, 512) for i in range(4)]
            passes = passes[:DBG.get("nchunks", 9)]
            if DBG["stop"] != "A":
                passes += [("B", "own", R_OWN + 512 * i, 512) for i in reversed(range(4))]

            def build_gens(p):
                mode, kind, r0, n = passes[p]
                gi, si = (2, 3) if kind == "ctx" else (0, 1)
                return [build_tile_gen(r0, j, gi, si, p % 2) for j in range(n // 128)]

            def split_even(items, k):
                out = [[] for _ in range(k)]
                for i, it in enumerate(items):
                    out[i * k // max(1, len(items))].append(it)
                return out

            def run_pass(p):
                mode, kind, r0, n = passes[p]
                nt = n // 128
                kt0 = r0 // 128
                hT, rhT = hxT[p % 2], R_hx[p % 2]
                next_gens = build_gens(p + 1) if p + 1 < len(passes) else []
                own_t0 = (r0 - R_OWN) // 128
                for j in range(nt):
                    tm_block(C_HGI, 512, hT, rhT, j)
                    P.op("act", lambda e, j=j: e.copy(out=vtm[:, j, :], in_=pM[:, 0:512]), reads=[R_pM], writes=[R_vtm[j]])
                att_gens = []
                groups = []
                if mode == "A":
                    rope = kind != "ctx"
                    if rope:
                        tcol = r0 - R_OTH
                        P.dma("sp", d_rope, rc[:, 0:n], ropec[:, tcol:tcol + n], writes=[R_rope])
                        P.dma("sp", d_rope2, rs_[:, 0:n], ropes[:, tcol:tcol + n], writes=[R_rope2])
                    for hh in range(4):
                        if kind == "own":
                            q_head(hh, hT, rhT, n)
                        gate_head(0, hh, hT, rhT, n)
                    for kvh in range(2):
                        att_gens.append(att_qk_gen(C_ATTK + kvh * 128, hT, rhT, n, 1, rope, kTd[:, kvh, r0:r0 + n], [R_k[kt0 + t] for t in range(nt)]))
                    for j in range(nt):
                        att_gens.append(v_gen(hT, rhT, j, kt0 + j))
                    if kind == "own":
                        qc = (r0 - R_OWN) // 512
                        for pr in range(4):
                            att_gens.append(att_qk_gen(C_ATTQ + pr * 128, hT, rhT, n, 0, True, qT[:, pr, qc * 512:(qc + 1) * 512], [R_q[2 * pr][qc], R_q[2 * pr + 1][qc]]))
                    for j in range(nt):
                        groups.append([hg_unit(0, hh, j, "store" if kind == "own" else None, own_t0 + j) for hh in range(4)])
                else:
                    for hh in range(4):
                        q_head(hh, hT, rhT, n)
                        g_head(hh, hT, rhT, n)
                        gate_head(1, hh, hT, rhT, n)
                    for j in reversed(range(nt)):
                        groups.append([hg_unit(1, hh, j, "final", own_t0 + j) for hh in range(4)])
                att_split = split_even(att_gens, len(groups))
                for gi_, g in enumerate(groups):
                    extra = []
                    if next_gens:
                        extra.append(next_gens.pop(0))
                    if att_split[gi_]:
                        extra.append(chain(att_split[gi_]))
                    lockstep(g + extra)
                if mode == "A" and kind == "ctx":
                    for hh in range(4):
                        gate_head(1, hh, hT, rhT, n)
                    for j in reversed(range(nt)):
                        lockstep([hg_unit(1, hh, j, None, 0) for hh in range(4)] + ([next_gens.pop(0)] if next_gens else []))
                if next_gens:
                    lockstep([chain(next_gens)])

            for u_ in range(NUB):
                P.op("pool", lambda e, u_=u_: e.memset(Abf[u_][:], 0.0), writes=[R_Abf[u_]])
            lockstep([chain(build_gens(0))])
            for p in range(len(passes)):
                if passes[p][0] == "B" and passes[p - 1][0] == "A":
                    dbg("kTd", kTd[:].rearrange("p a b -> p (a b)"), [128, 2 * NTOK], R_k, BF16)
                    dbg("qT", qT[:].rearrange("p a b -> p (a b)"), [128, 4 * 2048], [r for rr in R_q for r in rr], BF16)
                    dbg("oA", oA[:].rearrange("p a b -> p (a b)"), [128, 4 * 2048], [r for rr in R_oA for r in rr], BF16)
                    P.barrier()
                    for u_ in range(NUB):
                        P.op("pool", lambda e, u_=u_: e.memset(Abf[u_][:], 0.0), writes=[R_Abf[u_]])
                run_pass(p)
            if True:
                dbg("hgT", oA[:].rearrange("p a b -> p (a b)"), [128, 4 * 2048], [r for rr in R_oA for r in rr], BF16)
            P.barrier()

        if DBG["stop"] in ("A", "B"):
            print("nops at stop", P.nops)
            P.barrier()
            P.emit()
            return nc

        with ExitStack() as st:
            aoff[0] = mark1
            z_NSTG = 2
            z_stg = [sbuf(st, "sg%d" % i, [128, 8, 256], F32) for i in range(z_NSTG)]
            z_R_stg = [Res("sg%d" % i) for i in range(z_NSTG)]
            z_d_stg = [P.dma_sem() for _ in range(z_NSTG)]
            z_stg_i = [0]
            wout_bf = sbuf(st, "wout_bf", [128, 8, 1024], BF16)
            wdown_bf = sbuf(st, "wdown_bf", [128, NFT, 1024], BF16)
            R_wout = [Res("wout%d" % i) for i in range(4)]
            R_wdown = [Res("wdown%d" % i) for i in range(11)]
            mark3 = aoff[0]

            def load_piece2(src_ap):
                i = z_stg_i[0] % z_NSTG
                z_stg_i[0] += 1
                P.dma("sp", z_d_stg[i], z_stg[i][:], src_ap, writes=[z_R_stg[i]])
                return z_stg[i], z_R_stg[i]

            for pi in range(4):
                sg, rsg = load_piece2(wout[:, :, pi * 256:(pi + 1) * 256])
                P.op("pool", lambda e, sg=sg, pi=pi: e.tensor_copy(out=wout_bf[:, :, pi * 256:(pi + 1) * 256], in_=sg[:]), reads=[rsg], writes=[R_wout[pi]])
            for pi in range(11):
                sg, rsg = load_piece2(wdown[:, 2 * pi:2 * pi + 2, :].rearrange("p a (b c) -> p (a b) c", c=256))
                P.op("pool", lambda e, sg=sg, pi=pi: e.tensor_copy(out=wdown_bf[:, 2 * pi:2 * pi + 2, :].rearrange("p a (b c) -> p (a b) c", c=256), in_=sg[:]), reads=[rsg], writes=[R_wdown[pi]])

            wgus = nc.dram_tensor("wgus", [NFT, 128, 2048], BF16, kind="Internal").ap()
            R_wsc = [Res("wsc%d" % i) for i in range(NFT)]
            wcast = [sbuf(st, "wcast%d" % i, [128, 8, 256], BF16) for i in range(2)]
            R_wcast = [Res("wcast%d" % i) for i in range(2)]
            d_wst = [P.dma_sem() for _ in range(2)]
            for f in range(NFT):
                sg, rsg = load_piece2(wgu[f])
                P.op("pool", lambda e, sg=sg, f=f: e.tensor_copy(out=wcast[f % 2][:], in_=sg[:]), reads=[rsg], writes=[R_wcast[f % 2]])
                P.dma("sp", d_wst[f % 2], wgus[f], wcast[f % 2][:].rearrange("p a b -> p (a b)"), reads=[R_wcast[f % 2]], writes=[R_wsc[f]])

            with ExitStack() as sa:
                NPT = 6
                PT = [sbuf(sa, "PT%d" % i, [128, 512], BF16) for i in range(NPT)]
                R_PT = [Res("PT%d" % i) for i in range(NPT)]
                pS = [psum(sa, "pS%d" % i, [128, 512], F32) for i in range(NPT)]
                R_pS = [Res("pS%d" % i) for i in range(NPT)]
                pO = [psum(sa, "pO%d" % i, [128, 512], F32) for i in range(2)]
                R_pO = [Res("pO%d" % i) for i in range(2)]
                osb = [sbuf(sa, "osb%d" % i, [128, 512], F32) for i in range(2)]
                R_osb = [Res("osb%d" % i) for i in range(2)]
                rsum = [sbuf(sa, "rsum%d" % i, [128, 512], F32) for i in range(2)]
                R_rsum = [Res("rsum%d" % i) for i in range(2)]
                steps = [(qc, pr, kt) for qc in range(4) for pr in range(4) for kt in range(NKT)]
                LA = 2

                def emit_S(i):
                    qc, pr, kt = steps[i]
                    for half in range(2):
                        hd = 2 * pr + half
                        kvh = hd // 4
                        lo, hi = half * 64, half * 64 + 64
                        sb_ = (2 * i + half) % NPT
                        qsl = qT[lo:hi, pr, qc * 512:(qc + 1) * 512]
                        P.op("pe", lambda e, sb_=sb_, lo=lo, hi=hi, kvh=kvh, qsl=qsl: e.matmul(pS[sb_][:], lhsT=kTd[lo:hi, kvh, kt * 128:(kt + 1) * 128], rhs=qsl, start=True, stop=True),
                             reads=[R_k[kt], R_q[hd][qc]], writes=[R_pS[sb_]])
                    for half in range(2):
                        sb_ = (2 * i + half) % NPT
                        P.op("act", lambda e, sb_=sb_: e.activation(out=PT[sb_][:], in_=pS[sb_][:], func=AF.Exp, scale=0.125), reads=[R_pS[sb_]], writes=[R_PT[sb_]])

                def emit_PV(i):
                    qc, pr, kt = steps[i]
                    for half in range(2):
                        hd = 2 * pr + half
                        kvh = hd // 4
                        sb_ = (2 * i + half) % NPT
                        ob = half
                        if half == 0:
                            P.op("pe", lambda e, sb_=sb_, kvh=kvh: e.matmul(pO[0][:, :], lhsT=Vflat[:, (kt * 2 + kvh) * 130 + 64:(kt * 2 + kvh) * 130 + 192], rhs=PT[sb_][:], start=(kt == 0), stop=(kt == NKT - 1)),
                                 reads=[R_v[kt], R_PT[sb_]], writes=[R_pO[0]])
                        else:
                            P.op("pe", lambda e, sb_=sb_, kvh=kvh: e.matmul(pO[1][:, :], lhsT=Vaug[:, kt, kvh, 0:128], rhs=PT[sb_][:], start=(kt == 0), stop=(kt == NKT - 1)),
                                 reads=[R_v[kt], R_PT[sb_]], writes=[R_pO[1]])
                    if kt != NKT - 1:
                        return
                    z_pB, R_pB = pS[(2 * i) % NPT], R_pS[(2 * i) % NPT]
                    for half in range(2):
                        hd = 2 * pr + half
                        ob = half
                        lo, hi = half * 64, half * 64 + 64
                        qsl = qT[lo:hi, pr, qc * 512:(qc + 1) * 512]
                        srow = 64 if half == 0 else 0
                        nrow = 65 if half == 0 else 128
                        P.op("dve", lambda e, ob=ob, nrow=nrow: e.tensor_copy(out=osb[ob][0:nrow, :], in_=pO[ob][0:nrow, :]), reads=[R_pO[ob]], writes=[R_osb[ob]])
                        P.op("dve", lambda e, ob=ob, srow=srow: e.reciprocal(out=rsum[ob][srow:srow + 1, :], in_=osb[ob][srow:srow + 1, :]), reads=[R_osb[ob]], writes=[R_rsum[ob]])
                        P.op("pe", lambda e, ob=ob, srow=srow, z_pB=z_pB: e.matmul(z_pB[:], lhsT=ones32[srow:srow + 1, 0:128], rhs=rsum[ob][srow:srow + 1, :], start=True, stop=True),
                             reads=[R_rsum[ob], R_cst], writes=[R_pB])
                        P.op("dve", lambda e, ob=ob, lo=lo, hi=hi, qsl=qsl, z_pB=z_pB: e.tensor_tensor(out=qsl, in0=osb[ob][lo:hi, :], in1=z_pB[lo:hi, :], op=ALU.mult),
                             reads=[R_osb[ob], R_pB], writes=[R_q[hd][qc]])

                for i in range(len(steps) + LA):
                    if i < len(steps):
                        emit_S(i)
                    if i >= LA:
                        emit_PV(i - LA)
                dbg("attT", qT[:].rearrange("p a b -> p (a b)"), [128, 4 * 2048], [r for rr in R_q for r in rr], BF16)
                P.barrier()

            if DBG["stop"] == "C":
                P.barrier()
                P.emit()
                return nc

            with ExitStack() as sf:
                aoff[0] = 0
                actT = sbuf(sf, "actT", [128, NFT, 512], BF16)
                h2T = sbuf(sf, "h2T", [128, 8, 512], BF16)
                assert aoff[0] <= mark1
                aoff[0] = mark3
                z_gate_bc = sbuf(sf, "gate_bc4", [128, 2, 1024], F32)
                gfin_sb = sbuf(sf, "gfin_sb", [128, 1024], F32)
                z_R_gate = Res("gate4")
                P.dma("sp", P.dma_sem(), z_gate_bc[:].rearrange("p a b -> p (a b)"), gscr, reads=[R_gscr], writes=[z_R_gate])
                P.dma("sp", P.dma_sem(), gfin_sb[:], gfin, writes=[R_gfin])
                z_NXB = 1
                z_xt = [sbuf(sf, "fxt%d" % i, [128, 1024], F32) for i in range(z_NXB)]
                z_R_xt = [Res("fxt%d" % i) for i in range(z_NXB)]
                z_d_xt = [P.dma_sem() for _ in range(z_NXB)]
                x1 = sbuf(sf, "x1", [128, 4, 1024], F32)
                R_x1 = [Res("x1_%d" % i) for i in range(4)]
                z_junk = sbuf(sf, "fjunk", [128, 1024], BF16)
                z_R_junk = Res("fjunk")
                z_xs = [sbuf(sf, "fxs%d" % i, [128, 1024], BF16) for i in range(z_NXB)]
                z_R_xs = [Res("fxs%d" % i) for i in range(z_NXB)]
                z_stat = [sbuf(sf, "fstat%d" % i, [128, 4], F32) for i in range(z_NXB)]
                z_R_stat = [Res("fstat%d" % i) for i in range(z_NXB)]
                R_h2 = Res("h2T")
                R_act = [Res("act%d" % i) for i in range(NFT)]
                NWB = 3
                wgu_bf = [sbuf(sf, "wgu%d" % i, [128, 8, 256], BF16) for i in range(NWB)]
                R_wgu = [Res("wgu%d" % i) for i in range(NWB)]
                d_wld = [P.dma_sem() for _ in range(NWB)]
                fe = [sbuf(sf, "fe0", [128, 512], F32)] * 2
                R_fe = [Res("fe0")] * 2
                yo = [sbuf(sf, "yo%d" % i, [128, 1024], F32) for i in range(2)]
                R_yo = [Res("yo%d" % i) for i in range(2)]
                d_yo = [P.dma_sem() for _ in range(2)]
                pY = [psum(sf, "pY%d" % i, [128, 512], F32) for i in range(2)]
                R_pY = [Res("pY0"), Res("pY1")]
                z_pT = psum(sf, "fpT", [128, 1024], BF16)
                z_R_pT = Res("fpT")
                pA = [psum(sf, "pA%d" % i, [128, 512], F32) for i in range(2)]
                pBb = [psum(sf, "pBb%d" % i, [128, 512], F32) for i in range(2)]
                R_pA = [Res("pA0"), Res("pA1")]
                R_pBb = [Res("pBb0"), Res("pBb1")]
                xcnt = 0
                wcnt = 0
                ycnt = 0
                def wout_part(ci, j):
                    tile_own = ci * 4 + j
                    xb = 0
                    r0 = R_OWN + tile_own * 128
                    P.dma("sp", z_d_xt[xb], z_xt[xb][:], xin[r0:r0 + 128, :], writes=[z_R_xt[xb]])
                    for half in range(2):
                        for kc in range(8):
                            if kc < 4:
                                lhs = qT[:, kc, tile_own * 128:(tile_own + 1) * 128]
                                rl = [R_q[2 * kc][ci], R_q[2 * kc + 1][ci]]
                            else:
                                lhs = oA[:, kc - 4, tile_own * 128:(tile_own + 1) * 128]
                                rl = [R_oA[kc - 4][tile_own]]
                            P.op("pe", lambda e, lhs=lhs, kc=kc, half=half: e.matmul(pY[half][:], lhsT=lhs, rhs=wout_bf[:, kc, half * 512:(half + 1) * 512], start=(kc == 0), stop=(kc == 7)),
                                 reads=rl + [R_wout[half * 2], R_wout[half * 2 + 1]], writes=[R_pY[half]])
                        P.op("dve", lambda e, half=half, j=j: e.tensor_tensor(out=x1[:, j, half * 512:(half + 1) * 512], in0=pY[half][:], in1=z_gate_bc[:, 0, half * 512:(half + 1) * 512], op=ALU.mult),
                             reads=[R_pY[half], z_R_gate], writes=[R_x1[j]])
                    P.op("pool", lambda e, j=j, xb=xb: e.tensor_tensor(out=x1[:, j, :], in0=x1[:, j, :], in1=z_xt[xb][:], op=ALU.add), reads=[R_x1[j], z_R_xt[xb]], writes=[R_x1[j]])

                def norm_a(ci, j):
                    xb = 0
                    src = x1[:, j, :]
                    P.op("dve", lambda e, src=src, xb=xb: e.scalar_tensor_tensor(out=z_junk[:], in0=src, scalar=1.0, in1=src, op0=ALU.mult, op1=ALU.mult, accum_out=z_stat[xb][:, 0:1]),
                         reads=[R_x1[j]], writes=[z_R_junk, z_R_stat[xb]])
                    P.op("act", lambda e, xb=xb: e.activation(out=z_stat[xb][:, 1:2], in_=z_stat[xb][:, 0:1], func=AF.Ln, bias=EPS, scale=1.0 / D), reads=[z_R_stat[xb]], writes=[z_R_stat[xb]])
                    P.op("act", lambda e, xb=xb: e.activation(out=z_stat[xb][:, 2:3], in_=z_stat[xb][:, 1:2], func=AF.Exp, scale=-0.5), reads=[z_R_stat[xb]], writes=[z_R_stat[xb]])
                    P.op("act", lambda e, src=src, xb=xb: e.activation(out=z_xs[xb][:], in_=src, func=AF.Copy, scale=z_stat[xb][:, 2:3]), reads=[R_x1[j], z_R_stat[xb]], writes=[z_R_xs[xb]])

                def norm_b(ci, j):
                    xb = 0
                    for kc in range(8):
                        P.op("pe", lambda e, kc=kc, xb=xb: e.transpose(out=z_pT[:, kc * 128:(kc + 1) * 128], in_=z_xs[xb][:, kc * 128:(kc + 1) * 128], identity=ident),
                             reads=[z_R_xs[xb], R_cst], writes=[z_R_pT])
                    for kc in range(8):
                        o = h2T[:, kc, j * 128:(j + 1) * 128]
                        i_ = z_pT[:, kc * 128:(kc + 1) * 128]
                        if j % 2 == 0:
                            P.op("act", lambda e, o=o, i_=i_, kc=kc: e.activation(out=o, in_=i_, func=AF.Identity, bias=modv[:, 5, kc:kc + 1], scale=modv[:, 4, kc:kc + 1]),
                                 reads=[z_R_pT, R_modv], writes=[R_h2])
                        else:
                            P.op("dve", lambda e, o=o, i_=i_, kc=kc: e.tensor_scalar(out=o, in0=i_, scalar1=modv[:, 4, kc:kc + 1], scalar2=modv[:, 5, kc:kc + 1], op0=ALU.mult, op1=ALU.add),
                                 reads=[z_R_pT, R_modv], writes=[R_h2])


                for ci in range(4):
                    wout_part(ci, 0)
                    for j in range(4):
                        norm_a(ci, j)
                        if j < 3:
                            wout_part(ci, j + 1)
                        norm_b(ci, j)
                    if ci == 0:
                        dbg("x1", x1[:].rearrange("p a b -> p (a b)"), [128, 4096], R_x1)
                        dbg("h2T", h2T[:].rearrange("p a b -> p (a b)"), [128, 4096], [R_h2], BF16)
                    for f in range(NFT):
                        wb = wcnt % NWB
                        pb = wcnt % 2
                        wcnt += 1
                        P.dma("sp", d_wld[wb], wgu_bf[wb][:].rearrange("p a b -> p (a b)"), wgus[f], reads=[R_wsc[f]], writes=[R_wgu[wb]])
                        for kc in range(8):
                            P.op("pe", lambda e, kc=kc, wb=wb, pb=pb: e.matmul(pA[pb][:], lhsT=wgu_bf[wb][:, kc, 0:128], rhs=h2T[:, kc, :], start=(kc == 0), stop=(kc == 7)),
                                 reads=[R_wgu[wb], R_h2], writes=[R_pA[pb]])
                        for kc in range(8):
                            P.op("pe", lambda e, kc=kc, wb=wb, pb=pb: e.matmul(pBb[pb][:], lhsT=wgu_bf[wb][:, kc, 128:256], rhs=h2T[:, kc, :], start=(kc == 0), stop=(kc == 7)),
                                 reads=[R_wgu[wb], R_h2], writes=[R_pBb[pb]])
                        P.op("act", lambda e, pb=pb: e.activation(out=fe[pb][:], in_=pA[pb][:], func=AF.Silu), reads=[R_pA[pb]], writes=[R_fe[pb]])
                        P.op("dve", lambda e, pb=pb, f=f: e.tensor_tensor(out=actT[:, f, :], in0=pBb[pb][:], in1=fe[pb][:], op=ALU.mult), reads=[R_pBb[pb], R_fe[pb]], writes=[R_act[f]])
                    if ci == 0:
                        dbg("actT", actT[:].rearrange("p a b -> p (a b)"), [128, NFT * 512], R_act, BF16)
                    for j in range(4):
                        tile_own = ci * 4 + j
                        yb = ycnt % 2
                        ycnt += 1
                        for half in range(2):
                            for f in range(NFT):
                                P.op("pe", lambda e, f=f, half=half, j=j: e.matmul(pY[half][:], lhsT=actT[:, f, j * 128:(j + 1) * 128], rhs=wdown_bf[:, f, half * 512:(half + 1) * 512], start=(f == 0), stop=(f == NFT - 1)),
                                     reads=[R_act[f], R_wdown[f // 2]], writes=[R_pY[half]])
                            P.op("dve", lambda e, half=half, yb=yb: e.tensor_tensor(out=yo[yb][:, half * 512:(half + 1) * 512], in0=pY[half][:], in1=z_gate_bc[:, 1, half * 512:(half + 1) * 512], op=ALU.mult),
                                 reads=[R_pY[half], z_R_gate], writes=[R_yo[yb]])
                        P.op("pool", lambda e, j=j, yb=yb: e.tensor_tensor(out=yo[yb][:], in0=yo[yb][:], in1=x1[:, j, :], op=ALU.add), reads=[R_yo[yb], R_x1[j]], writes=[R_yo[yb]])
                        sb2 = 0
                        P.op("dve", lambda e, yb=yb, sb2=sb2: e.scalar_tensor_tensor(out=z_junk[:], in0=yo[yb][:], scalar=1.0, in1=yo[yb][:], op0=ALU.mult, op1=ALU.mult, accum_out=z_stat[sb2][:, 0:1]),
                             reads=[R_yo[yb]], writes=[z_R_junk, z_R_stat[sb2]])
                        P.op("act", lambda e, sb2=sb2: e.activation(out=z_stat[sb2][:, 1:2], in_=z_stat[sb2][:, 0:1], func=AF.Ln, bias=EPS, scale=1.0 / D), reads=[z_R_stat[sb2]], writes=[z_R_stat[sb2]])
                        P.op("act", lambda e, sb2=sb2: e.activation(out=z_stat[sb2][:, 2:3], in_=z_stat[sb2][:, 1:2], func=AF.Exp, scale=-0.5), reads=[z_R_stat[sb2]], writes=[z_R_stat[sb2]])
                        P.op("dve", lambda e, yb=yb, sb2=sb2: e.scalar_tensor_tensor(out=yo[yb][:], in0=yo[yb][:], scalar=z_stat[sb2][:, 2:3], in1=gfin_sb[:], op0=ALU.mult, op1=ALU.mult),
                             reads=[R_yo[yb], z_R_stat[sb2], R_gfin], writes=[R_yo[yb]])
                        P.dma("sp", d_yo[yb], y[tile_own * 128:(tile_own + 1) * 128, :], yo[yb][:], reads=[R_yo[yb]])
                P.barrier()
        P.barrier()
        P.emit()
    return nc


def _rope_tables():
    n_rows = 4096 // 64
    row = np.repeat(np.arange(n_rows, dtype=np.float32), 64)
    col = np.tile(np.arange(64, dtype=np.float32), n_rows)
    inv = (10000.0 ** (-np.arange(0, 32, 2, dtype=np.float32) / 32)).astype(np.float32)
    ar = row[:, None] * inv
    ac = col[:, None] * inv
    ang = np.concatenate([ar, ar, ac, ac], axis=-1)
    return np.cos(ang).astype(np.float32), np.sin(ang).astype(np.float32)


def _consts():
    ident = np.eye(128, dtype=np.float32)
    blk = np.zeros((128, 128), np.float32)
    blk[:64, :64] = 1.0 / 64
    blk[64:, 64:] = 1.0 / 64
    perm = np.zeros((128, 128), np.float32)
    for m in range(128):
        if (m % 32) < 16:
            perm[m + 16, m] = -1.0
        else:
            perm[m - 16, m] = 1.0
    ones = np.full((128, 128), 1.0 / 128, np.float32)
    s = np.arange(128)[:, None]
    t = np.arange(128)[None, :]
    maskf = (s <= t).astype(np.float32)
    maskb = (s >= t).astype(np.float32)
    return np.concatenate([ident, blk, perm, ones, maskf, maskb], axis=1)


def _kc(w):
    K, N = w.shape
    return np.ascontiguousarray(w.reshape(K // 128, 128, N).transpose(1, 0, 2))


def prepare_inputs(x, c, ctx, c_ctx, w_mod, b_mod, g_norm1, w_in, g_q, g_k, lb_raw, g_hg, w_out,
                   g_norm2, w_gu, w_down, g_final):
    f = lambda a: np.asarray(a, dtype=np.float32)
    x, c, ctx, c_ctx = f(x), f(c), f(ctx), f(c_ctx)
    w_mod, b_mod, w_in, w_out, w_gu, w_down = f(w_mod)[0], f(b_mod)[0], f(w_in)[0], f(w_out)[0], f(w_gu)[0], f(w_down)[0]
    g1, g2, gq, gk, ghg, gfin = f(g_norm1)[0], f(g_norm2)[0], f(g_q)[0], f(g_k)[0], f(g_hg)[0], f(g_final)
    lb_raw = f(lb_raw)
    cos, sin = _rope_tables()
    cst = _consts()
    wmod_l = _kc(w_mod)
    bsel = np.concatenate([b_mod[s * 1024:(s + 1) * 1024] for s in (0, 1, 3, 4)])
    bmodT = np.repeat(bsel.reshape(32, 128).T[:, :, None], 2, axis=2).reshape(128, 64)
    bmodrep = np.ascontiguousarray(np.broadcast_to(np.concatenate([b_mod[2048:3072], b_mod[5120:6144]])[None, :], (128, 2048)))
    gn = np.concatenate([g1.reshape(8, 128).T, g2.reshape(8, 128).T], axis=1)
    gfin_rep = np.ascontiguousarray(np.broadcast_to(gfin[None, :], (128, 1024)))
    gsm = np.stack([np.tile(gq, 2), np.tile(gk, 2), ghg], axis=1)
    wout_l = _kc(w_out)
    wgu_l = np.stack([np.concatenate([_kc(w_gu[:, j * 128:(j + 1) * 128]), _kc(w_gu[:, DFF + j * 128: DFF + (j + 1) * 128])], axis=2) for j in range(NFT)])
    wdown_l = _kc(w_down)
    maps = []
    for b in range(4):
        for h in range(2):
            if h == 1:
                own = np.arange(2048, 4096)
                oth = np.arange(0, 2048)
                cidx = np.arange(256)
                fA, fB, dA, dB = slice(1792, 2304), slice(2304, 2816), 0, 1
            else:
                own = np.arange(2047, -1, -1)
                oth = np.arange(4095, 2047, -1)
                cidx = np.arange(255, -1, -1)
                fA, fB, dA, dB = slice(2304, 2816), slice(1792, 2304), 1, 0
            xin = np.concatenate([ctx[b][cidx], x[b][oth], x[b][own]], axis=0)
            tl = np.concatenate([oth, own])
            ropec = np.ascontiguousarray(np.tile(cos[tl].T, (2, 1)))
            ropes = np.ascontiguousarray(np.tile(sin[tl].T, (2, 1)))
            cT2 = np.stack([c[b].reshape(8, 128).T, c_ctx.reshape(8, 128).T], axis=2).reshape(128, 16)
            cTrep = np.repeat(c[b].reshape(8, 128).T[:, :, None], 128, axis=2).reshape(128, 1024)
            k0, k1 = w_in[:, 512:576], w_in[:, 576:640]
            win_l = np.concatenate([w_in[:, 0:512], k0, k0, k1, k1, w_in[:, 640:768], w_in[:, 768:1280], w_in[:, 1280:1792],
                                    w_in[:, fA], w_in[:, fB], w_in[:, 2816:3328]], axis=1)
            lbl = np.stack([lb_raw[:, dA, :], lb_raw[:, dB, :]], axis=1)
            lbl = lbl.reshape(2, 2, 4, 128).transpose(3, 0, 1, 2).reshape(128, 16)
            maps.append({
                "xin": np.ascontiguousarray(xin), "ropec": ropec, "ropes": ropes,
                "cT2": np.ascontiguousarray(cT2), "cTrep": np.ascontiguousarray(cTrep),
                "wmod": wmod_l, "bmodT": np.ascontiguousarray(bmodT), "bmodrep": bmodrep,
                "gn": np.ascontiguousarray(gn), "gfin": gfin_rep, "gsm": np.ascontiguousarray(gsm),
                "lbraw": np.ascontiguousarray(lbl), "win": _kc(win_l), "wout": wout_l, "wgu": wgu_l,
                "wdown": wdown_l, "cst": cst,
            })
    return maps


def assemble(results):
    out = np.empty((4, 4096, 1024), np.float32)
    i = 0
    for b in range(4):
        for h in range(2):
            yl = np.asarray(results[i]["y"], dtype=np.float32)
            if h == 1:
                out[b, 2048:4096] = yl
            else:
                out[b, 0:2048] = yl[::-1]
            i += 1
    return out


def kernel(**inputs):
    maps = prepare_inputs(**inputs)
    nc = build()
    res = run_bass_kernel_spmd(nc, maps, core_ids=list(range(8)))
    return assemble(res.results)
```

```python
import numpy as np
from contextlib import ExitStack
import concourse.bass as bass
import concourse.mybir as mybir
from concourse.bass_utils import run_bass_kernel_spmd

F32 = mybir.dt.float32
BF16 = mybir.dt.bfloat16
AF = mybir.ActivationFunctionType
ALU = mybir.AluOpType

ENGS = ("pe", "act", "dve", "pool", "sp")
DBG = {"stop": None, "dump": False}


class Res:
    __slots__ = ("name", "w", "r", "excl")

    def __init__(self, name="", excl=False):
        self.name = name
        self.w = None
        self.r = {}
        self.excl = excl or name.startswith(("p", "z_p", "fp"))


class Prog:
    def __init__(self, nc, stack):
        self.nc = nc
        self.stack = stack
        self.thunks = {e: [] for e in ENGS}
        self.count = {e: 0 for e in ENGS}
        self.seen = {e: {} for e in ENGS}
        self.sems = {}
        for e in ENGS:
            self.sems[e] = stack.enter_context(nc.semaphore("s_" + e))
        self.dma_cnt = {}
        self.n_dsem = 0

    def dma_sem(self):
        self.n_dsem += 1
        key = "d%d" % self.n_dsem
        self.sems[key] = self.stack.enter_context(self.nc.semaphore(key))
        self.dma_cnt[key] = 0
        return key

    def _deps(self, eng, reads, writes):
        deps = {}

        def add(k, v):
            if v > deps.get(k, 0):
                deps[k] = v
        for r in reads:
            if r.w is not None:
                add(*r.w)
            if r.excl:
                for k, v in r.r.items():
                    if k != eng:
                        add(k, v)
        for w in writes:
            if w.w is not None:
                add(*w.w)
            for k, v in w.r.items():
                add(k, v)
        waits = []
        for k, v in deps.items():
            if k == eng and eng == "pe":
                continue
            if self.seen[eng].get(k, 0) >= v:
                continue
            self.seen[eng][k] = v
            waits.append((k, v))
        return waits

    def op(self, eng, fn, reads=(), writes=()):
        self.nops = getattr(self, "nops", 0) + 1
        if self.nops > DBG.get("limit", 10 ** 9):
            return 0
        waits = self._deps(eng, reads, writes)
        self.count[eng] += 1
        n = self.count[eng]
        sems = self.sems

        def thunk(e, waits=waits, fn=fn, eng=eng):
            for k, v in waits:
                e.wait_ge(sems[k], v)
            fn(e).then_inc(sems[eng], 1)
        self.thunks[eng].append(thunk)
        for r in reads:
            if r.r.get(eng, 0) < n:
                r.r[eng] = n
        for w in writes:
            w.w = (eng, n)
            w.r = {}
        return n

    def dma(self, q, key, out, in_, reads=(), writes=()):
        self.nops = getattr(self, "nops", 0) + 1
        if self.nops > DBG.get("limit", 10 ** 9):
            return 0
        waits = self._deps(q, reads, writes)
        self.dma_cnt[key] += 16
        v = self.dma_cnt[key]
        assert v < 60000
        sems = self.sems

        def thunk(e, waits=waits):
            for k, vv in waits:
                e.wait_ge(sems[k], vv)
            e.dma_start(out=out, in_=in_).then_inc(sems[key], 16)
        self.thunks[q].append(thunk)
        for r in reads:
            if r.r.get(key, 0) < v:
                r.r[key] = v
        for w in writes:
            w.w = (key, v)
            w.r = {}
        return v

    def barrier(self):
        for eng in ENGS:
            waits = []
            for k in list(self.sems.keys()):
                v = self.count[k] if k in self.count else self.dma_cnt[k]
                if k == eng or v == 0:
                    continue
                if self.seen[eng].get(k, 0) >= v:
                    continue
                self.seen[eng][k] = v
                waits.append((k, v))
            sems = self.sems

            def thunk(e, waits=waits):
                for k, vv in waits:
                    e.wait_ge(sems[k], vv)
            self.thunks[eng].append(thunk)

    def emit(self):
        nc = self.nc
        th = self.thunks
        with nc.Block() as block:
            @block.tensor
            def _(e):
                for t in th["pe"]:
                    t(e)

            @block.scalar
            def _(e):
                for t in th["act"]:
                    t(e)

            @block.vector
            def _(e):
                for t in th["dve"]:
                    t(e)

            @block.gpsimd
            def _(e):
                for t in th["pool"]:
                    t(e)

            @block.sync
            def _(e):
                for t in th["sp"]:
                    t(e)


D = 1024
NTOK = 4352
R_CTX, R_OTH, R_OWN = 0, 256, 2304
NKT = NTOK // 128
NCOL = 3328 + 128
C_ATTQ, C_ATTK, C_ATTV, C_HGQ, C_HGI, C_FA, C_FB, C_G = 0, 512, 768, 896, 1408, 1920, 2432, 2944
DFF = 2816
NFT = 22
EPS = 1e-6


def build(dump=None):
    dump = dump or {}
    nc = bass.Bass("TRN2", target_bir_lowering=False)

    def din(name, shape, dt=F32):
        return nc.dram_tensor(name, list(shape), dt, kind="ExternalInput").ap()

    xin = din("xin", [NTOK, D])
    ropec = din("ropec", [128, 4096])
    ropes = din("ropes", [128, 4096])
    cT2 = din("cT2", [128, 16])
    cTrep = din("cTrep", [128, 1024])
    wmod = din("wmod", [128, 8, 6144])
    bmodT = din("bmodT", [128, 64])
    bmodrep = din("bmodrep", [128, 2048])
    gn = din("gn", [128, 16])
    gfin = din("gfin", [128, 1024])
    gsm = din("gsm", [128, 3])
    lbraw = din("lbraw", [128, 16])
    win = din("win", [128, 8, NCOL])
    wout = din("wout", [128, 8, 1024])
    wgu = din("wgu", [NFT, 128, 8, 256])
    wdown = din("wdown", [128, NFT, 1024])
    cst = din("cst", [128, 6 * 128])
    y = nc.dram_tensor("y", [2048, D], F32, kind="ExternalOutput").ap()
    dbg_outs = {}

    with ExitStack() as gst:
        P = Prog(nc, gst)

        def rsbuf(st, name, shape, dt):
            return st.enter_context(nc.sbuf_tensor(name, list(shape), dt))

        ARENA = 170 * 1024
        arena_t = gst.enter_context(nc.sbuf_tensor("arena", [128, ARENA // 2], BF16))
        aoff = [0]

        def sbuf(st, name, shape, dt):
            elems = 1
            for d_ in shape[1:]:
                elems *= d_
            nb = elems * (4 if dt == F32 else 2)
            nb_al = (nb + 63) // 64 * 64
            o = aoff[0]
            assert o + nb_al <= ARENA, (name, o, nb_al)
            ap = arena_t[:, o // 2:(o + nb) // 2]
            if dt == F32:
                ap = ap.bitcast(F32)
            if len(shape) == 3:
                ap = ap.rearrange("p (a b) -> p a b", a=shape[1])
            elif len(shape) == 4:
                ap = ap.rearrange("p (a b c) -> p a b c", a=shape[1], b=shape[2])
            aoff[0] = o + nb_al
            return ap

        def psum(st, name, shape, dt):
            return st.enter_context(nc.psum_tensor(name, list(shape), dt))

        d_out = P.dma_sem()

        def dbg(name, ap, shape, res, dt=F32):
            if name not in dump:
                return
            t = nc.dram_tensor("dbg_" + name, list(shape), dt, kind="ExternalOutput").ap()
            P.dma("sp", d_out, t, ap, reads=res)

        cstb = rsbuf(gst, "cstb", [128, 768], BF16)
        ones32 = rsbuf(gst, "ones32", [128, 512], F32)
        modv = rsbuf(gst, "modv", [128, 6, 8], F32)
        gsm_sb = rsbuf(gst, "gsm_sb", [128, 3], F32)
        lbv = rsbuf(gst, "lbv", [128, 4, 8], F32)
        qT = rsbuf(gst, "qT", [128, 4, 2048], BF16)
        oA = rsbuf(gst, "oA", [128, 4, 2048], BF16)
        kTd = sbuf(gst, "kTd", [128, 2, NTOK], BF16)
        Vaug = sbuf(gst, "Vaug", [128, NKT + 1, 2, 130], BF16)
        Vflat = Vaug.rearrange("p a b c -> p (a b c)")
        mark1 = aoff[0]
        gscr = nc.dram_tensor("gscr", [128, 2048], F32, kind="Internal").ap()
        d_gs = P.dma_sem()
        R_gscr = Res("gscr")
        R_cst = Res("cst")
        R_modv = Res("modv")
        R_gate = Res("gate")
        R_gfin = Res("gfin")
        R_gsm = Res("gsm")
        R_lbv = Res("lbv")
        R_q = [[Res("q%d_%d" % (hd, qc)) for qc in range(4)] for hd in range(8)]
        R_k = [Res("k%d" % t) for t in range(NKT)]
        R_v = [Res("v%d" % t) for t in range(NKT)]
        R_oA = [[Res("oA%d_%d" % (hh, t)) for t in range(16)] for hh in range(4)]
        ident = cstb[:, 0:128]
        blk64 = cstb[:, 128:256]
        permT = cstb[:, 256:384]
        ones128 = cstb[:, 384:512]
        maskf = cstb[:, 512:640]
        maskb = cstb[:, 640:768]

        d_c = P.dma_sem()
        P.op("pool", lambda e: e.memset(ones32[:], 1.0), writes=[R_cst])
        P.dma("sp", P.dma_sem(), gsm_sb[:], gsm, writes=[R_gsm])

        with ExitStack() as st:
            win_bf = sbuf(st, "win_bf", [128, 8, NCOL], BF16)
            mark2 = aoff[0]
            cst32 = sbuf(st, "cst32", [128, 768], F32)
            R_cst32 = Res("cst32")
            P.dma("sp", P.dma_sem(), cst32[:], cst, writes=[R_cst32])
            P.op("dve", lambda e: e.tensor_copy(out=cstb[:], in_=cst32[:]), reads=[R_cst32], writes=[R_cst])
            NSTG = 2
            stg = [sbuf(st, "stg%d" % i, [128, 8, 512], F32) for i in range(NSTG)]
            R_stg = [Res("stg%d" % i) for i in range(NSTG)]
            d_stg = [P.dma_sem() for _ in range(NSTG)]
            stg_i = [0]

            def load_piece(src_ap, w):
                i = stg_i[0] % NSTG
                stg_i[0] += 1
                P.dma("sp", d_stg[i], stg[i][:, :, 0:w], src_ap, writes=[R_stg[i]])
                return stg[i], R_stg[i]

            NWP = (NCOL + 511) // 512
            R_win = [Res("win%d" % i) for i in range(NWP)]

            cast_engs = ["pool", "dve", "act"]
            win_done = set()

            def win_piece(pi):
                if pi >= NWP or pi in win_done:
                    return
                win_done.add(pi)
                c0 = pi * 512
                w = min(512, NCOL - c0)
                sg, rsg = load_piece(win[:, :, c0:c0 + w], w)
                eng = cast_engs[pi % 3]
                if eng == "act":
                    P.op("act", lambda e: e.copy(out=win_bf[:, :, c0:c0 + w], in_=sg[:, :, 0:w]), reads=[rsg], writes=[R_win[pi]])
                else:
                    P.op(eng, lambda e: e.tensor_copy(out=win_bf[:, :, c0:c0 + w], in_=sg[:, :, 0:w]), reads=[rsg], writes=[R_win[pi]])

            with ExitStack() as sm:
                c2 = sbuf(sm, "c2", [128, 16], F32)
                e2 = sbuf(sm, "e2", [128, 16], F32)
                sc2 = sbuf(sm, "sc2", [128, 16], F32)
                crep = sbuf(sm, "crep", [128, 1024], F32)
                erep = sbuf(sm, "erep", [128, 1024], F32)
                screp = sbuf(sm, "screp", [128, 1024], F32)
                bT = sbuf(sm, "bT", [128, 64], F32)
                brep = sbuf(sm, "brep", [128, 2048], F32)
                gn_sb = sbuf(sm, "gn_sb", [128, 16], F32)
                lbr = sbuf(sm, "lbr", [128, 16], F32)
                modfm = sbuf(sm, "modfm", [128, 64], F32)
                modrow = sbuf(sm, "modrow", [128, 4096], F32)
                R_modrow = Res("modrow")
                gate_bc = sbuf(sm, "gate_bc", [128, 2, 1024], F32)
                pmod = psum(sm, "pmod", [128, 64], F32)
                pg = psum(sm, "pg", [128, 512], F32)
                R_c2, R_crep, R_bT, R_brep, R_gn, R_lbr, R_modfm, R_pmod, R_pg = [Res(n) for n in
                    ("c2", "crep", "bT", "brep", "gn", "lbr", "modfm", "pmod", "pg")]
                R_e2, R_sc2, R_erep, R_screp = Res("e2"), Res("sc2"), Res("erep"), Res("screp")
                P.dma("sp", P.dma_sem(), c2[:], cT2, writes=[R_c2])
                P.dma("sp", P.dma_sem(), crep[:], cTrep, writes=[R_crep])
                P.dma("sp", P.dma_sem(), bT[:], bmodT, writes=[R_bT])
                P.dma("sp", P.dma_sem(), brep[:], bmodrep, writes=[R_brep])
                P.dma("sp", P.dma_sem(), gn_sb[:], gn, writes=[R_gn])
                P.dma("sp", P.dma_sem(), lbr[:], lbraw, writes=[R_lbr])
                P.op("act", lambda e: e.activation(out=e2[:], in_=c2[:], func=AF.Exp, scale=-1.0), reads=[R_c2], writes=[R_e2])
                P.op("dve", lambda e: e.tensor_scalar(out=e2[:], in0=e2[:], scalar1=1.0, scalar2=None, op0=ALU.add), reads=[R_e2], writes=[R_e2])
                P.op("dve", lambda e: e.reciprocal(out=e2[:], in_=e2[:]), reads=[R_e2], writes=[R_e2])
                P.op("dve", lambda e: e.tensor_tensor(out=sc2[:], in0=c2[:], in1=e2[:], op=ALU.mult), reads=[R_c2, R_e2], writes=[R_sc2])
                P.op("act", lambda e: e.activation(out=erep[:], in_=crep[:], func=AF.Exp, scale=-1.0), reads=[R_crep], writes=[R_erep])
                P.op("dve", lambda e: e.tensor_scalar(out=erep[:], in0=erep[:], scalar1=1.0, scalar2=None, op0=ALU.add), reads=[R_erep], writes=[R_erep])
                P.op("dve", lambda e: e.reciprocal(out=erep[:], in_=erep[:]), reads=[R_erep], writes=[R_erep])
                P.op("dve", lambda e: e.tensor_tensor(out=screp[:], in0=crep[:], in1=erep[:], op=ALU.mult), reads=[R_crep, R_erep], writes=[R_screp])
                P.op("dve", lambda e: e.tensor_tensor(out=lbv[:, 3, :], in0=lbr[:, 8:16], in1=lbr[:, 0:8], op=ALU.subtract), reads=[R_lbr], writes=[R_lbv])
                P.op("act", lambda e: e.activation(out=lbv[:, 3, :], in_=lbv[:, 3, :], func=AF.Exp), reads=[R_lbv], writes=[R_lbv])
                P.op("dve", lambda e: e.tensor_scalar(out=lbv[:, 3, :], in0=lbv[:, 3, :], scalar1=1.0, scalar2=None, op0=ALU.add), reads=[R_lbv], writes=[R_lbv])
                P.op("dve", lambda e: e.reciprocal(out=lbv[:, 0, :], in_=lbv[:, 3, :]), reads=[R_lbv], writes=[R_lbv])
                P.op("dve", lambda e: e.tensor_scalar(out=lbv[:, 1, :], in0=lbv[:, 0, :], scalar1=-1.0, scalar2=1.0, op0=ALU.mult, op1=ALU.add), reads=[R_lbv], writes=[R_lbv])
                P.op("dve", lambda e: e.tensor_scalar(out=lbv[:, 2, :], in0=lbv[:, 1, :], scalar1=-1.0, scalar2=None, op0=ALU.mult), reads=[R_lbv], writes=[R_lbv])
                for si, split in enumerate((0, 1, 3, 4)):
                    for half in range(2):
                        c0 = split * 1024 + half * 512
                        pc_ = si * 2 + half
                        sg, rsg = load_piece(wmod[:, :, c0:c0 + 512], 512)
                        for kc in range(8):
                            P.op("pe", lambda e, sg=sg, kc=kc: e.matmul(pg[0:2, :], lhsT=sc2[:, 2 * kc:2 * kc + 2], rhs=sg[:, kc, :],
                                                                        start=(kc == 0), stop=(kc == 7)), reads=[rsg, R_sc2], writes=[R_pg])
                        P.op("dve", lambda e, pc_=pc_: e.tensor_copy(out=modrow[0:2, pc_ * 512:(pc_ + 1) * 512], in_=pg[0:2, :]), reads=[R_pg], writes=[R_modrow])
                        win_piece(pc_)
                for j in range(32):
                    P.op("pe", lambda e, j=j: e.transpose(out=pmod[:, 2 * j:2 * j + 2], in_=modrow[0:2, j * 128:(j + 1) * 128], identity=cst32[0:2, 0:2]),
                         reads=[R_modrow, R_cst32], writes=[R_pmod])
                P.op("dve", lambda e: e.tensor_tensor(out=modfm[:], in0=pmod[:], in1=bT[:], op=ALU.add), reads=[R_pmod, R_bT], writes=[R_modfm])
                mf = modfm[:].rearrange("p (j v) -> p j v", v=2)
                P.op("dve", lambda e: e.scalar_tensor_tensor(out=modv[:, 0, :], in0=mf[:, 8:16, 0], scalar=1.0, in1=gn_sb[:, 0:8], op0=ALU.add, op1=ALU.mult), reads=[R_modfm, R_gn], writes=[R_modv])
                P.op("dve", lambda e: e.tensor_copy(out=modv[:, 1, :], in_=mf[:, 0:8, 0]), reads=[R_modfm], writes=[R_modv])
                P.op("dve", lambda e: e.scalar_tensor_tensor(out=modv[:, 2, :], in0=mf[:, 8:16, 1], scalar=1.0, in1=gn_sb[:, 0:8], op0=ALU.add, op1=ALU.mult), reads=[R_modfm, R_gn], writes=[R_modv])
                P.op("dve", lambda e: e.tensor_copy(out=modv[:, 3, :], in_=mf[:, 0:8, 1]), reads=[R_modfm], writes=[R_modv])
                P.op("dve", lambda e: e.scalar_tensor_tensor(out=modv[:, 4, :], in0=mf[:, 24:32, 0], scalar=1.0, in1=gn_sb[:, 8:16], op0=ALU.add, op1=ALU.mult), reads=[R_modfm, R_gn], writes=[R_modv])
                P.op("dve", lambda e: e.tensor_copy(out=modv[:, 5, :], in_=mf[:, 16:24, 0]), reads=[R_modfm], writes=[R_modv])
                for gi, split in enumerate((2, 5)):
                    for half in range(2):
                        c0 = split * 1024 + half * 512
                        sg, rsg = load_piece(wmod[:, :, c0:c0 + 512], 512)
                        for kc in range(8):
                            P.op("pe", lambda e, sg=sg, kc=kc: e.matmul(
                                pg[:], lhsT=screp[:, kc * 128:(kc + 1) * 128], rhs=sg[:, kc, :],
                                start=(kc == 0), stop=(kc == 7)), reads=[rsg, R_screp], writes=[R_pg])
                        P.op("dve", lambda e, gi=gi, half=half: e.tensor_tensor(
                            out=gate_bc[:, gi, half * 512:(half + 1) * 512], in0=pg[:],
                            in1=brep[:, gi * 1024 + half * 512: gi * 1024 + (half + 1) * 512], op=ALU.add),
                            reads=[R_pg, R_brep], writes=[R_gate])
                dbg("modv", modv[:].rearrange("p a b -> p (a b)"), [128, 48], [R_modv])
                dbg("gate", gate_bc[:].rearrange("p a b -> p (a b)"), [128, 2048], [R_gate])
                dbg("lbv", lbv[:].rearrange("p a b -> p (a b)"), [128, 32], [R_lbv])
                P.dma("sp", d_gs, gscr, gate_bc[:].rearrange("p a b -> p (a b)"), reads=[R_gate], writes=[R_gscr])

            for pi in range(NWP):
                win_piece(pi)

            def rwin(c0, ncols=128):
                return [R_win[i] for i in range(c0 // 512, (c0 + ncols - 1) // 512 + 1)]

            P.barrier()
            aoff[0] = mark2
            if DBG["stop"] == "M":
                print("nops", P.nops)
                P.emit()
                return nc

            NXB = 1
            xt = [sbuf(st, "xt%d" % i, [128, 1024], F32) for i in range(NXB)]
            R_xt = [Res("xt%d" % i) for i in range(NXB)]
            d_xt = [P.dma_sem() for _ in range(NXB)]
            xs = [sbuf(st, "xs%d" % i, [128, 1024], BF16) for i in range(NXB)]
            R_xs = [Res("xs%d" % i) for i in range(NXB)]
            stat = [sbuf(st, "stat%d" % i, [128, 4], F32) for i in range(NXB)]
            R_stat = [Res("stat%d" % i) for i in range(NXB)]
            hxT = [sbuf(st, "hxT%d" % i, [128, 8, 512], BF16) for i in range(2)]
            R_hx = [Res("hx%d" % i) for i in range(2)]
            markA = aoff[0]
            rc = sbuf(st, "rc", [128, 512], F32)
            rs_ = sbuf(st, "rs", [128, 512], F32)
            R_rope = Res("ropec")
            R_rope2 = Res("ropes")
            d_rope = P.dma_sem()
            d_rope2 = P.dma_sem()
            xsq = sbuf(st, "xsq", [128, 512], BF16)
            xg = sbuf(st, "xg", [128, 512], BF16)
            lnv = sbuf(st, "lnv", [128, 512], F32)
            t1 = sbuf(st, "t1", [128, 512], F32)
            R_xsq, R_xg, R_lnv, R_t1 = Res("xsq"), Res("xg"), Res("lnv"), Res("t1")
            endA = aoff[0]
            aoff[0] = markA
            NUB = 4
            hg_gate = [sbuf(st, "hgg%d" % i, [128, 512], F32) for i in range(4)]
            R_hgg = [Res("hgg%d" % i) for i in range(4)]
            osum = [sbuf(st, "osum%d" % i, [128, 128], F32) for i in range(NUB)]
            osq = [sbuf(st, "osq%d" % i, [128, 128], BF16) for i in range(NUB)]
            olr = [sbuf(st, "olr%d" % i, [128, 128], F32) for i in range(NUB)]
            R_osum = [Res("osum%d" % i) for i in range(NUB)]
            R_osq = [Res("osq%d" % i) for i in range(NUB)]
            R_olr = [Res("olr%d" % i) for i in range(NUB)]
            aoff[0] = max(endA, aoff[0])
            hq = [sbuf(st, "hq%d" % i, [128, 512], F32) for i in range(4)]
            he = [sbuf(st, "he%d" % i, [128, 512], F32) for i in range(1)] * 2
            hk = [sbuf(st, "hk%d" % i, [128, 512], F32) for i in range(4)]
            hB = [sbuf(st, "hB%d" % i, [128, 516], F32) for i in range(4)]
            nBm = [sbuf(st, "nBm%d" % i, [128, 4], F32) for i in range(4)]
            R_hq = [Res("hq%d" % i) for i in range(4)]
            R_he = [Res("he0")] * 4
            R_hk = [Res("hk%d" % i) for i in range(4)]
            R_hl = [Res("hl0")] * 4
            R_hB = [Res("hB%d" % i) for i in range(4)]
            R_hnB = [Res("nBm%d" % i) for i in range(4)]
            he = he * 2
            t2 = he[0]
            R_t2 = R_he[0]
            vtm = sbuf(st, "vtm", [128, 4, 512], BF16)
            R_vtm = [Res("vtm%d" % i) for i in range(4)]
            Epos = [sbuf(st, "Epos%d" % i, [128, 132], F32) for i in range(NUB)]
            Eneg = [sbuf(st, "Eneg%d" % i, [128, 132], F32) for i in range(NUB)]
            Qh = [sbuf(st, "Qh%d" % i, [128, 128], BF16) for i in range(NUB)]
            KhT = [sbuf(st, "KhT%d" % i, [128, 128], BF16) for i in range(NUB)]
            Khtm = [sbuf(st, "Khtm%d" % i, [128, 128], BF16) for i in range(NUB)]
            Abf = [sbuf(st, "Abf%d" % i, [128, 128], BF16) for i in range(NUB)]
            Smid = [sbuf(st, "Smid%d" % i, [128, 128], F32) for i in range(NUB)]
            Smidb = [sbuf(st, "Smidb%d" % i, [128, 128], BF16) for i in range(NUB)]
            R_E = [Res("E%d" % i) for i in range(NUB)]
            R_Qh = [Res("Qh%d" % i) for i in range(NUB)]
            R_KhT = [Res("KhT%d" % i) for i in range(NUB)]
            R_Khtm = [Res("Khtm%d" % i) for i in range(NUB)]
            R_Abf = [Res("Abf%d" % i) for i in range(NUB)]
            R_Smid = [Res("Smid%d" % i) for i in range(NUB)]
            R_Smidb = [Res("Smidb%d" % i) for i in range(NUB)]
            S32 = sbuf(st, "S32", [128, 8, 128], F32)
            R_S = [Res("S%d" % i) for i in range(8)]
            pT = psum(st, "pT", [128, 1024], BF16)
            pF = [psum(st, "pF%d" % i, [128, 512], F32) for i in range(2)]
            pM = psum(st, "pM", [128, 512], F32)
            pSS = psum(st, "pSS", [128, 512], F32)
            pROT = psum(st, "pROT", [128, 512], F32)
            pH = [psum(st, "pH%d" % i, [128, 512], F32) for i in range(2)]
            R_pT, R_pM, R_pSS, R_pROT = Res("pT"), Res("pM"), Res("pSS"), Res("pROT")
            R_pF = [Res("pF0"), Res("pF1")]
            R_pHA = [Res("pH0"), Res("pH1")]
            R_pHO = R_pHA
            R_pHU = R_pHA
            R_pHK = R_pHA

            P.op("pool", lambda e: e.memset(S32[:], 0.0), writes=R_S)
            for i in range(4):
                P.op("pool", lambda e, i=i: e.memset(hB[i][:, 0:1], 0.0), writes=[R_hB[i]])
            P.op("pool", lambda e: e.memset(Vaug[:, NKT, :, :], 0.0), writes=R_v)
            P.op("pool", lambda e: e.memset(Vaug[:, :, :, 0:64], 0.0), writes=R_v)
            P.op("pool", lambda e: e.memset(Vaug[:, :, :, 0:1], 1.0), writes=R_v)
            P.op("pool", lambda e: e.memset(Vaug[:, :, :, 128:129], 1.0), writes=R_v)
            P.op("pool", lambda e: e.memset(Vaug[:, :, :, 129:130], 0.0), writes=R_v)

            cnt = {"x": 0, "hx": 0, "pf": 0, "u": 0, "ph": 0}
            print("nops before sweep", P.nops)

            def build_hxT(r0, n, gi, si):
                hb = cnt["hx"] % 2
                cnt["hx"] += 1
                for j in range(n // 128):
                    xb = cnt["x"] % NXB
                    cnt["x"] += 1
                    P.dma("sp", d_xt[xb], xt[xb][:], xin[r0 + 128 * j: r0 + 128 * (j + 1), :], writes=[R_xt[xb]])
                    norm_T(xt[xb][:], R_xt[xb], xb, gi, si, hxT[hb], R_hx[hb], j * 128)
                return hxT[hb], R_hx[hb]

            def norm_T(src, rsrc, xb, gi, si, hT, rhT, col0):
                P.op("dve", lambda e: e.scalar_tensor_tensor(out=xs[xb][:], in0=src, scalar=1.0, in1=src, op0=ALU.mult, op1=ALU.mult,
                                                              accum_out=stat[xb][:, 0:1]), reads=[rsrc], writes=[R_xs[xb], R_stat[xb]])
                P.op("act", lambda e: e.activation(out=stat[xb][:, 1:2], in_=stat[xb][:, 0:1], func=AF.Ln, bias=EPS, scale=1.0 / D), reads=[R_stat[xb]], writes=[R_stat[xb]])
                P.op("act", lambda e: e.activation(out=stat[xb][:, 2:3], in_=stat[xb][:, 1:2], func=AF.Exp, scale=-0.5), reads=[R_stat[xb]], writes=[R_stat[xb]])
                P.op("act", lambda e: e.activation(out=xs[xb][:], in_=src, func=AF.Copy, scale=stat[xb][:, 2:3]), reads=[rsrc, R_stat[xb]], writes=[R_xs[xb]])
                for kc in range(8):
                    P.op("pe", lambda e, kc=kc: e.transpose(out=pT[:, kc * 128:(kc + 1) * 128], in_=xs[xb][:, kc * 128:(kc + 1) * 128], identity=ident),
                         reads=[R_xs[xb], R_cst], writes=[R_pT])
                for kc in range(8):
                    o = hT[:, kc, col0:col0 + 128]
                    i_ = pT[:, kc * 128:(kc + 1) * 128]
                    if (col0 // 128) % 2 == 0:
                        P.op("act", lambda e, o=o, i_=i_, kc=kc: e.activation(out=o, in_=i_, func=AF.Identity, bias=modv[:, si, kc:kc + 1], scale=modv[:, gi, kc:kc + 1]),
                             reads=[R_pT, R_modv], writes=[rhT])
                    else:
                        P.op("dve", lambda e, o=o, i_=i_, kc=kc: e.tensor_scalar(out=o, in0=i_, scalar1=modv[:, gi, kc:kc + 1], scalar2=modv[:, si, kc:kc + 1], op0=ALU.mult, op1=ALU.add),
                             reads=[R_pT, R_modv], writes=[rhT])

            def fm_block(c0, hT, rhT, n):
                pb = cnt["pf"] % 2
                cnt["pf"] += 1
                for kc in range(8):
                    P.op("pe", lambda e, kc=kc: e.matmul(pF[pb][:, 0:n], lhsT=win_bf[:, kc, c0:c0 + 128], rhs=hT[:, kc, 0:n], start=(kc == 0), stop=(kc == 7)),
                         reads=rwin(c0) + [rhT], writes=[R_pF[pb]])
                return pF[pb], R_pF[pb]

            def att_qk_block(c0, hT, rhT, n, gcol, rope, dst, rdst):
                X, rX = fm_block(c0, hT, rhT, n)
                P.op("act", lambda e: e.activation(out=xsq[:, 0:n], in_=X[:, 0:n], func=AF.Square), reads=[rX], writes=[R_xsq])
                P.op("dve", lambda e: e.tensor_scalar(out=xg[:, 0:n], in0=X[:, 0:n], scalar1=gsm_sb[:, gcol:gcol + 1], scalar2=None, op0=ALU.mult), reads=[rX, R_gsm], writes=[R_xg])
                P.op("pe", lambda e: e.matmul(pSS[:, 0:n], lhsT=blk64, rhs=xsq[:, 0:n], start=True, stop=True), reads=[R_xsq, R_cst], writes=[R_pSS])
                P.op("act", lambda e: e.activation(out=lnv[:, 0:n], in_=pSS[:, 0:n], func=AF.Ln, bias=EPS, scale=1.0), reads=[R_pSS], writes=[R_lnv])
                P.op("act", lambda e: e.activation(out=lnv[:, 0:n], in_=lnv[:, 0:n], func=AF.Exp, scale=-0.5), reads=[R_lnv], writes=[R_lnv])
                if rope:
                    P.op("pe", lambda e: e.matmul(pROT[:, 0:n], lhsT=permT, rhs=xg[:, 0:n], start=True, stop=True), reads=[R_xg, R_cst], writes=[R_pROT])
                    P.op("dve", lambda e: e.tensor_tensor(out=t1[:, 0:n], in0=xg[:, 0:n], in1=rc[:, 0:n], op=ALU.mult), reads=[R_xg, R_rope], writes=[R_t1])
                    P.op("dve", lambda e: e.tensor_tensor(out=t2[:, 0:n], in0=pROT[:, 0:n], in1=rs_[:, 0:n], op=ALU.mult), reads=[R_pROT, R_rope2], writes=[R_t2])
                    P.op("dve", lambda e: e.tensor_tensor(out=t1[:, 0:n], in0=t1[:, 0:n], in1=t2[:, 0:n], op=ALU.add), reads=[R_t1, R_t2], writes=[R_t1])
                    P.op("pool", lambda e: e.tensor_tensor(out=dst, in0=t1[:, 0:n], in1=lnv[:, 0:n], op=ALU.mult), reads=[R_t1, R_lnv], writes=rdst)
                else:
                    P.op("pool", lambda e: e.tensor_tensor(out=dst, in0=xg[:, 0:n], in1=lnv[:, 0:n], op=ALU.mult), reads=[R_xg, R_lnv], writes=rdst)

            def tm_block(c0, ncols, hT, rhT, j):
                for kc in range(8):
                    P.op("pe", lambda e, kc=kc: e.matmul(pM[:, 0:ncols], lhsT=hT[:, kc, j * 128:(j + 1) * 128], rhs=win_bf[:, kc, c0:c0 + ncols], start=(kc == 0), stop=(kc == 7)),
                         reads=rwin(c0, ncols) + [rhT], writes=[R_pM])

            def gate_head(d, hh, hT, rhT, n):
                b = hh
                c0 = (C_FA if d == 0 else C_FB) + hh * 128
                X, rX = fm_block(c0, hT, rhT, n)
                li = d * 4 + hh
                P.op("act", lambda e: e.activation(out=he[b][:, 0:n], in_=X[:, 0:n], func=AF.Exp, scale=-1.0), reads=[rX], writes=[R_he[b]])
                P.op("act", lambda e: e.activation(out=he[b][:, 0:n], in_=he[b][:, 0:n], func=AF.Ln, bias=1.0, scale=1.0), reads=[R_he[b]], writes=[R_he[b]])
                P.op("act", lambda e: e.activation(out=he[b][:, 0:n], in_=he[b][:, 0:n], func=AF.Exp, scale=-1.0), reads=[R_he[b]], writes=[R_he[b]])
                P.op("dve", lambda e: e.tensor_scalar(out=hk[b][:, 0:n], in0=he[b][:, 0:n], scalar1=lbv[:, 2, li:li + 1], scalar2=lbv[:, 1, li:li + 1], op0=ALU.mult, op1=ALU.add),
                     reads=[R_he[b], R_lbv], writes=[R_hk[b]])
                P.op("act", lambda e: e.activation(out=he[b][:, 0:n], in_=he[b][:, 0:n], func=AF.Ln, bias=lbv[:, 0, li:li + 1], scale=lbv[:, 1, li:li + 1]),
                     reads=[R_he[b], R_lbv], writes=[R_he[b]])
                P.op("dve", lambda e: e.tensor_tensor_scan(out=hB[b][:, 1:n + 1], data0=ones32[:, 0:n], data1=he[b][:, 0:n], initial=0.0, op0=ALU.mult, op1=ALU.add),
                     reads=[R_he[b], R_cst], writes=[R_hB[b]])
                P.op("dve", lambda e: e.tensor_scalar(out=nBm[b][:, 0:n // 128], in0=hB[b][:, 0:n].rearrange("p (t c) -> p t c", c=128)[:, :, 64], scalar1=-1.0, scalar2=None, op0=ALU.mult),
                     reads=[R_hB[b]], writes=[R_hnB[b]])

            def q_head(hh, hT, rhT, n):
                b = hh
                X, rX = fm_block(C_HGQ + hh * 128, hT, rhT, n)
                P.op("act", lambda e: e.activation(out=hq[b][:, 0:n], in_=X[:, 0:n], func=AF.Exp, scale=-1.0), reads=[rX], writes=[R_hq[b]])
                P.op("act", lambda e: e.activation(out=hq[b][:, 0:n], in_=hq[b][:, 0:n], func=AF.Ln, bias=1.0, scale=1.0), reads=[R_hq[b]], writes=[R_hq[b]])
                P.op("act", lambda e: e.activation(out=hq[b][:, 0:n], in_=hq[b][:, 0:n], func=AF.Exp, scale=-1.0), reads=[R_hq[b]], writes=[R_hq[b]])
                P.op("dve", lambda e: e.scalar_tensor_tensor(out=hq[b][:, 0:n], in0=X[:, 0:n], scalar=128.0 ** -0.5, in1=hq[b][:, 0:n], op0=ALU.mult, op1=ALU.mult),
                     reads=[rX, R_hq[b]], writes=[R_hq[b]])

            def g_head(hh, hT, rhT, n):
                b = hh
                X, rX = fm_block(C_G + hh * 128, hT, rhT, n)
                P.op("act", lambda e: e.activation(out=hg_gate[b][:, 0:n], in_=X[:, 0:n], func=AF.Exp, scale=-1.0), reads=[rX], writes=[R_hgg[b]])
                P.op("act", lambda e: e.activation(out=hg_gate[b][:, 0:n], in_=hg_gate[b][:, 0:n], func=AF.Ln, bias=1.0, scale=1.0), reads=[R_hgg[b]], writes=[R_hgg[b]])
                P.op("act", lambda e: e.activation(out=hg_gate[b][:, 0:n], in_=hg_gate[b][:, 0:n], func=AF.Exp, scale=-1.0), reads=[R_hgg[b]], writes=[R_hgg[b]])
                P.op("dve", lambda e: e.scalar_tensor_tensor(out=hg_gate[b][:, 0:n], in0=X[:, 0:n], scalar=gsm_sb[:, 2:3], in1=hg_gate[b][:, 0:n], op0=ALU.mult, op1=ALU.mult),
                     reads=[rX, R_hgg[b], R_gsm], writes=[R_hgg[b]])

            def hg_unit(d, hh, j, out_mode, tile_own):
                b = hh
                u = hh
                ph = hh
                ts = j * 128
                sidx = d * 4 + hh
                Aps = pH[ph][:, 0:128]
                Ops = pH[ph][:, 128:256]
                Ups = pH[ph][:, 256:384]
                P.op("act", lambda e: e.activation(out=Epos[u][:, 0:129], in_=hB[b][:, ts:ts + 129], func=AF.Exp, bias=nBm[b][:, j:j + 1], scale=1.0),
                     reads=[R_hB[b], R_hnB[b]], writes=[R_E[u]])
                P.op("act", lambda e: e.activation(out=Eneg[u][:, 0:129], in_=hB[b][:, ts:ts + 129], func=AF.Exp, bias=hB[b][:, ts + 64:ts + 65], scale=-1.0),
                     reads=[R_hB[b]], writes=[R_E[u]])
                yield
                if d == 0:
                    ep, en = Epos[u][:, 1:129], Eneg[u][:, 1:129]
                    dmid, dout = Eneg[u][:, 0:1], Epos[u][:, 128:129]
                else:
                    ep, en = Eneg[u][:, 0:128], Epos[u][:, 0:128]
                    dmid, dout = Epos[u][:, 128:129], Eneg[u][:, 0:1]
                P.op("pool", lambda e: e.tensor_tensor(out=KhT[u][:], in0=hk[b][:, ts:ts + 128], in1=en, op=ALU.mult), reads=[R_hk[b], R_E[u]], writes=[R_KhT[u]])
                P.op("dve", lambda e: e.tensor_scalar(out=Smid[u][:], in0=S32[:, sidx, :], scalar1=dmid, scalar2=None, op0=ALU.mult), reads=[R_S[sidx], R_E[u]], writes=[R_Smid[u]])
                if out_mode is not None:
                    P.op("pool", lambda e: e.tensor_tensor(out=Qh[u][:], in0=hq[b][:, ts:ts + 128], in1=ep, op=ALU.mult), reads=[R_hq[b], R_E[u]], writes=[R_Qh[u]])
                    P.op("act", lambda e: e.copy(out=Smidb[u][:], in_=Smid[u][:]), reads=[R_Smid[u]], writes=[R_Smidb[u]])
                yield
                P.op("pe", lambda e: e.transpose(out=pHK[ph], in_=KhT[u][:], identity=ident), reads=[R_KhT[u], R_cst], writes=[R_pHK[ph]])
                if out_mode is not None:
                    if d == 0:
                        P.op("pe", lambda e: e.matmul(Aps[0:64, 0:128], lhsT=KhT[u][:, 0:64], rhs=Qh[u][:, 0:128], start=True, stop=True), reads=[R_KhT[u], R_Qh[u]], writes=[R_pHA[ph]])
                        P.op("pe", lambda e: e.matmul(Aps[64:128, 64:128], lhsT=KhT[u][:, 64:128], rhs=Qh[u][:, 64:128], start=True, stop=True), reads=[R_KhT[u], R_Qh[u]], writes=[R_pHA[ph]])
                    else:
                        P.op("pe", lambda e: e.matmul(Aps[64:128, 0:128], lhsT=KhT[u][:, 64:128], rhs=Qh[u][:, 0:128], start=True, stop=True), reads=[R_KhT[u], R_Qh[u]], writes=[R_pHA[ph]])
                        P.op("pe", lambda e: e.matmul(Aps[0:64, 0:64], lhsT=KhT[u][:, 0:64], rhs=Qh[u][:, 0:64], start=True, stop=True), reads=[R_KhT[u], R_Qh[u]], writes=[R_pHA[ph]])
                yield
                P.op("act", lambda e: e.copy(out=Khtm[u][:], in_=pHK[ph]), reads=[R_pHK[ph]], writes=[R_Khtm[u]])
                if out_mode is not None:
                    if d == 0:
                        P.op("dve", lambda e: e.tensor_tensor(out=Abf[u][0:64, 0:128], in0=Aps[0:64, 0:128], in1=maskf[0:64, 0:128], op=ALU.mult), reads=[R_pHA[ph], R_cst], writes=[R_Abf[u]])
                        P.op("dve", lambda e: e.tensor_tensor(out=Abf[u][64:128, 64:128], in0=Aps[64:128, 64:128], in1=maskf[64:128, 64:128], op=ALU.mult), reads=[R_pHA[ph], R_cst], writes=[R_Abf[u]])
                    else:
                        P.op("dve", lambda e: e.tensor_tensor(out=Abf[u][64:128, 0:128], in0=Aps[64:128, 0:128], in1=maskb[64:128, 0:128], op=ALU.mult), reads=[R_pHA[ph], R_cst], writes=[R_Abf[u]])
                        P.op("dve", lambda e: e.tensor_tensor(out=Abf[u][0:64, 0:64], in0=Aps[0:64, 0:64], in1=maskb[0:64, 0:64], op=ALU.mult), reads=[R_pHA[ph], R_cst], writes=[R_Abf[u]])
                yield
                if out_mode is not None:
                    P.op("pe", lambda e: e.matmul(Ops, lhsT=Smidb[u][:], rhs=Qh[u][:], start=True, stop=False), reads=[R_Smidb[u], R_Qh[u]], writes=[R_pHO[ph]])
                    P.op("pe", lambda e: e.matmul(Ops, lhsT=vtm[:, j, hh * 128:(hh + 1) * 128], rhs=Abf[u][:], start=False, stop=True), reads=[R_vtm[j], R_Abf[u]], writes=[R_pHO[ph]])
                P.op("pe", lambda e: e.matmul(Ups, lhsT=Khtm[u][:], rhs=vtm[:, j, hh * 128:(hh + 1) * 128], start=True, stop=True), reads=[R_Khtm[u], R_vtm[j]], writes=[R_pHU[ph]])
                yield
                P.op("dve", lambda e: e.tensor_tensor(out=Smid[u][:], in0=Ups, in1=Smid[u][:], op=ALU.add), reads=[R_pHU[ph], R_Smid[u]], writes=[R_Smid[u]])
                if out_mode == "store":
                    P.op("act", lambda e: e.copy(out=oA[:, hh, tile_own * 128:(tile_own + 1) * 128], in_=Ops), reads=[R_pHO[ph]], writes=[R_oA[hh][tile_own]])
                elif out_mode == "final":
                    osl = oA[:, hh, tile_own * 128:(tile_own + 1) * 128]
                    P.op("dve", lambda e: e.tensor_tensor(out=osum[u][:], in0=Ops, in1=osl, op=ALU.add), reads=[R_pHO[ph], R_oA[hh][tile_own]], writes=[R_osum[u]])
                yield
                P.op("dve", lambda e: e.tensor_scalar(out=S32[:, sidx, :], in0=Smid[u][:], scalar1=dout, scalar2=None, op0=ALU.mult), reads=[R_Smid[u], R_E[u]], writes=[R_S[sidx]])
                if out_mode == "final":
                    P.op("pool", lambda e: e.tensor_tensor(out=osq[u][:], in0=osum[u][:], in1=osum[u][:], op=ALU.mult), reads=[R_osum[u]], writes=[R_osq[u]])
                    yield
                    P.op("pe", lambda e: e.matmul(Aps, lhsT=ones128, rhs=osq[u][:], start=True, stop=True), reads=[R_osq[u], R_cst], writes=[R_pHA[ph]])
                    yield
                    P.op("act", lambda e: e.activation(out=olr[u][:], in_=Aps, func=AF.Ln, bias=EPS, scale=1.0), reads=[R_pHA[ph]], writes=[R_olr[u]])
                    yield
                    P.op("act", lambda e: e.activation(out=olr[u][:], in_=olr[u][:], func=AF.Exp, scale=-0.5), reads=[R_olr[u]], writes=[R_olr[u]])
                    yield
                    P.op("pool", lambda e: e.tensor_tensor(out=osum[u][:], in0=osum[u][:], in1=olr[u][:], op=ALU.mult), reads=[R_osum[u], R_olr[u]], writes=[R_osum[u]])
                    yield
                    P.op("pool", lambda e: e.tensor_tensor(out=osl, in0=osum[u][:], in1=hg_gate[b][:, ts:ts + 128], op=ALU.mult), reads=[R_osum[u], R_hgg[b]], writes=[R_oA[hh][tile_own]])

            def lockstep(gens):
                gens = list(gens)
                while gens:
                    nxt = []
                    for g in gens:
                        try:
                            next(g)
                            nxt.append(g)
                        except StopIteration:
                            pass
                    gens = nxt

            pH = pH + [pSS, pROT]
            R_pHA += [R_pSS, R_pROT]
            pHK = [pH[i][:, 384:512].bitcast(BF16)[:, 0:128] for i in range(4)]


            def chain(gens):
                for g in gens:
                    yield from g

            def build_tile_gen(r0, j, gi, si, hb):
                xb = 0
                hT, rhT = hxT[hb], R_hx[hb]
                col0 = j * 128
                src, rsrc = xt[xb][:], R_xt[xb]
                P.dma("sp", d_xt[xb], xt[xb][:], xin[r0 + 128 * j: r0 + 128 * (j + 1), :], writes=[R_xt[xb]])
                P.op("dve", lambda e: e.scalar_tensor_tensor(out=xs[xb][:], in0=src, scalar=1.0, in1=src, op0=ALU.mult, op1=ALU.mult,
                                                              accum_out=stat[xb][:, 0:1]), reads=[rsrc], writes=[R_xs[xb], R_stat[xb]])
                yield
                P.op("act", lambda e: e.activation(out=stat[xb][:, 1:2], in_=stat[xb][:, 0:1], func=AF.Ln, bias=EPS, scale=1.0 / D), reads=[R_stat[xb]], writes=[R_stat[xb]])
                yield
                P.op("act", lambda e: e.activation(out=stat[xb][:, 2:3], in_=stat[xb][:, 1:2], func=AF.Exp, scale=-0.5), reads=[R_stat[xb]], writes=[R_stat[xb]])
                yield
                P.op("act", lambda e: e.activation(out=xs[xb][:], in_=src, func=AF.Copy, scale=stat[xb][:, 2:3]), reads=[rsrc, R_stat[xb]], writes=[R_xs[xb]])
                yield
                for kc in range(8):
                    P.op("pe", lambda e, kc=kc: e.transpose(out=pT[:, kc * 128:(kc + 1) * 128], in_=xs[xb][:, kc * 128:(kc + 1) * 128], identity=ident),
                         reads=[R_xs[xb], R_cst], writes=[R_pT])
                yield
                for kc in range(8):
                    o = hT[:, kc, col0:col0 + 128]
                    i_ = pT[:, kc * 128:(kc + 1) * 128]
                    if j % 2 == 0:
                        P.op("act", lambda e, o=o, i_=i_, kc=kc: e.activation(out=o, in_=i_, func=AF.Identity, bias=modv[:, si, kc:kc + 1], scale=modv[:, gi, kc:kc + 1]),
                             reads=[R_pT, R_modv], writes=[rhT])
                    else:
                        P.op("dve", lambda e, o=o, i_=i_, kc=kc: e.tensor_scalar(out=o, in0=i_, scalar1=modv[:, gi, kc:kc + 1], scalar2=modv[:, si, kc:kc + 1], op0=ALU.mult, op1=ALU.add),
                             reads=[R_pT, R_modv], writes=[rhT])

            def att_qk_gen(c0, hT, rhT, n, gcol, rope, dst, rdst):
                X, rX = fm_block(c0, hT, rhT, n)
                ob_ = 1 - ((cnt["pf"] - 1) % 2)
                pSSl, R_pSSl = pF[ob_], R_pF[ob_]
                pROTl, R_pROTl = pM, R_pM
                yield
                P.op("act", lambda e: e.activation(out=xsq[:, 0:n], in_=X[:, 0:n], func=AF.Square), reads=[rX], writes=[R_xsq])
                P.op("dve", lambda e: e.tensor_scalar(out=xg[:, 0:n], in0=X[:, 0:n], scalar1=gsm_sb[:, gcol:gcol + 1], scalar2=None, op0=ALU.mult), reads=[rX, R_gsm], writes=[R_xg])
                yield
                P.op("pe", lambda e: e.matmul(pSSl[:, 0:n], lhsT=blk64, rhs=xsq[:, 0:n], start=True, stop=True), reads=[R_xsq, R_cst], writes=[R_pSSl])
                yield
                P.op("act", lambda e: e.activation(out=lnv[:, 0:n], in_=pSSl[:, 0:n], func=AF.Ln, bias=EPS, scale=1.0), reads=[R_pSSl], writes=[R_lnv])
                if rope:
                    P.op("pe", lambda e: e.matmul(pROTl[:, 0:n], lhsT=permT, rhs=xg[:, 0:n], start=True, stop=True), reads=[R_xg, R_cst], writes=[R_pROTl])
                    P.op("dve", lambda e: e.tensor_tensor(out=t1[:, 0:n], in0=xg[:, 0:n], in1=rc[:, 0:n], op=ALU.mult), reads=[R_xg, R_rope], writes=[R_t1])
                yield
                P.op("act", lambda e: e.activation(out=lnv[:, 0:n], in_=lnv[:, 0:n], func=AF.Exp, scale=-0.5), reads=[R_lnv], writes=[R_lnv])
                if rope:
                    P.op("dve", lambda e: e.tensor_tensor(out=t2[:, 0:n], in0=pROTl[:, 0:n], in1=rs_[:, 0:n], op=ALU.mult), reads=[R_pROTl, R_rope2], writes=[R_t2])
                    yield
                    P.op("dve", lambda e: e.tensor_tensor(out=t1[:, 0:n], in0=t1[:, 0:n], in1=t2[:, 0:n], op=ALU.add), reads=[R_t1, R_t2], writes=[R_t1])
                    yield
                    P.op("pool", lambda e: e.tensor_tensor(out=dst, in0=t1[:, 0:n], in1=lnv[:, 0:n], op=ALU.mult), reads=[R_t1, R_lnv], writes=rdst)
                else:
                    yield
                    P.op("pool", lambda e: e.tensor_tensor(out=dst, in0=xg[:, 0:n], in1=lnv[:, 0:n], op=ALU.mult), reads=[R_xg, R_lnv], writes=rdst)

            def v_gen(hT, rhT, j, kt):
                tm_block(C_ATTV, 128, hT, rhT, j)
                yield
                P.op("act", lambda e: e.copy(out=Vaug[:, kt, :, 64:128], in_=pM[:, 0:128].rearrange("p (k d) -> p k d", k=2)), reads=[R_pM], writes=[R_v[kt]])

            passes = [("A", "ctx", R_CTX, 256)] + [("A", "oth", R_OTH + 512 * i, 512) for i in range(4)] + \
                     [("A", "own", R_OWN + 512 * i, 512) for i in range(4)]
            passes = passes[:DBG.get("nchunks", 9)]
            if DBG["stop"] != "A":
                passes += [("B", "own", R_OWN + 512 * i, 512) for i in reversed(range(4))]

            def build_gens(p):
                mode, kind, r0, n = passes[p]
                gi, si = (2, 3) if kind == "ctx" else (0, 1)
                return [build_tile_gen(r0, j, gi, si, p % 2) for j in range(n // 128)]

            def split_even(items, k):
                out = [[] for _ in range(k)]
                for i, it in enumerate(items):
                    out[i * k // max(1, len(items))].append(it)
                return out

            def run_pass(p):
                mode, kind, r0, n = passes[p]
                nt = n // 128
                kt0 = r0 // 128
                hT, rhT = hxT[p % 2], R_hx[p % 2]
                next_gens = build_gens(p + 1) if p + 1 < len(passes) else []
                own_t0 = (r0 - R_OWN) // 128
                for j in range(nt):
                    tm_block(C_HGI, 512, hT, rhT, j)
                    P.op("act", lambda e, j=j: e.copy(out=vtm[:, j, :], in_=pM[:, 0:512]), reads=[R_pM], writes=[R_vtm[j]])
                att_gens = []
                groups = []
                if mode == "A":
                    rope = kind != "ctx"
                    if rope:
                        tcol = r0 - R_OTH
                        P.dma("sp", d_rope, rc[:, 0:n], ropec[:, tcol:tcol + n], writes=[R_rope])
                        P.dma("sp", d_rope2, rs_[:, 0:n], ropes[:, tcol:tcol + n], writes=[R_rope2])
                    for hh in range(4):
                        if kind == "own":
                            q_head(hh, hT, rhT, n)
                        gate_head(0, hh, hT, rhT, n)
                    for kvh in range(2):
                        att_gens.append(att_qk_gen(C_ATTK + kvh * 128, hT, rhT, n, 1, rope, kTd[:, kvh, r0:r0 + n], [R_k[kt0 + t] for t in range(nt)]))
                    for j in range(nt):
                        att_gens.append(v_gen(hT, rhT, j, kt0 + j))
                    if kind == "own":
                        qc = (r0 - R_OWN) // 512
                        for pr in range(4):
                            att_gens.append(att_qk_gen(C_ATTQ + pr * 128, hT, rhT, n, 0, True, qT[:, pr, qc * 512:(qc + 1) * 512], [R_q[2 * pr][qc], R_q[2 * pr + 1][qc]]))
                    for j in range(nt):
                        groups.append([hg_unit(0, hh, j, "store" if kind == "own" else None, own_t0 + j) for hh in range(4)])
                else:
                    for hh in range(4):
                        q_head(hh, hT, rhT, n)
                        g_head(hh, hT, rhT, n)
                        gate_head(1, hh, hT, rhT, n)
                    for j in reversed(range(nt)):
                        groups.append([hg_unit(1, hh, j, "final", own_t0 + j) for hh in range(4)])
                att_split = split_even(att_gens, len(groups))
                for gi_, g in enumerate(groups):
                    extra = []
                    if next_gens:
                        extra.append(next_gens.pop(0))
                    if att_split[gi_]:
                        extra.append(chain(att_split[gi_]))
                    lockstep(g + extra)
                if mode == "A" and kind == "ctx":
                    for hh in range(4):
                        gate_head(1, hh, hT, rhT, n)
                    for j in reversed(range(nt)):
                        lockstep([hg_unit(1, hh, j, None, 0) for hh in range(4)] + ([next_gens.pop(0)] if next_gens else []))
                if next_gens:
                    lockstep([chain(next_gens)])

            for u_ in range(NUB):
                P.op("pool", lambda e, u_=u_: e.memset(Abf[u_][:], 0.0), writes=[R_Abf[u_]])
            lockstep([chain(build_gens(0))])
            for p in range(len(passes)):
                if passes[p][0] == "B" and passes[p - 1][0] == "A":
                    dbg("kTd", kTd[:].rearrange("p a b -> p (a b)"), [128, 2 * NTOK], R_k, BF16)
                    dbg("qT", qT[:].rearrange("p a b -> p (a b)"), [128, 4 * 2048], [r for rr in R_q for r in rr], BF16)
                    dbg("oA", oA[:].rearrange("p a b -> p (a b)"), [128, 4 * 2048], [r for rr in R_oA for r in rr], BF16)
                    P.barrier()
                    for u_ in range(NUB):
                        P.op("pool", lambda e, u_=u_: e.memset(Abf[u_][:], 0.0), writes=[R_Abf[u_]])
                run_pass(p)
            if True:
                dbg("hgT", oA[:].rearrange("p a b -> p (a b)"), [128, 4 * 2048], [r for rr in R_oA for r in rr], BF16)
            P.barrier()

        if DBG["stop"] in ("A", "B"):
            print("nops at stop", P.nops)
            P.barrier()
            P.emit()
            return nc

        with ExitStack() as st:
            aoff[0] = mark1
            z_NSTG = 2
            z_stg = [sbuf(st, "sg%d" % i, [128, 8, 256], F32) for i in range(z_NSTG)]
            z_R_stg = [Res("sg%d" % i) for i in range(z_NSTG)]
            z_d_stg = [P.dma_sem() for _ in range(z_NSTG)]
            z_stg_i = [0]
            wout_bf = sbuf(st, "wout_bf", [128, 8, 1024], BF16)
            wdown_bf = sbuf(st, "wdown_bf", [128, NFT, 1024], BF16)
            R_wout = [Res("wout%d" % i) for i in range(4)]
            R_wdown = [Res("wdown%d" % i) for i in range(11)]
            mark3 = aoff[0]

            def load_piece2(src_ap):
                i = z_stg_i[0] % z_NSTG
                z_stg_i[0] += 1
                P.dma("sp", z_d_stg[i], z_stg[i][:], src_ap, writes=[z_R_stg[i]])
                return z_stg[i], z_R_stg[i]

            for pi in range(4):
                sg, rsg = load_piece2(wout[:, :, pi * 256:(pi + 1) * 256])
                P.op("pool", lambda e, sg=sg, pi=pi: e.tensor_copy(out=wout_bf[:, :, pi * 256:(pi + 1) * 256], in_=sg[:]), reads=[rsg], writes=[R_wout[pi]])
            for pi in range(11):
                sg, rsg = load_piece2(wdown[:, 2 * pi:2 * pi + 2, :].rearrange("p a (b c) -> p (a b) c", c=256))
                P.op("pool", lambda e, sg=sg, pi=pi: e.tensor_copy(out=wdown_bf[:, 2 * pi:2 * pi + 2, :].rearrange("p a (b c) -> p (a b) c", c=256), in_=sg[:]), reads=[rsg], writes=[R_wdown[pi]])

            wgus = nc.dram_tensor("wgus", [NFT, 128, 2048], BF16, kind="Internal").ap()
            R_wsc = [Res("wsc%d" % i) for i in range(NFT)]
            wcast = [sbuf(st, "wcast%d" % i, [128, 8, 256], BF16) for i in range(2)]
            R_wcast = [Res("wcast%d" % i) for i in range(2)]
            d_wst = [P.dma_sem() for _ in range(2)]
            for f in range(NFT):
                sg, rsg = load_piece2(wgu[f])
                P.op("pool", lambda e, sg=sg, f=f: e.tensor_copy(out=wcast[f % 2][:], in_=sg[:]), reads=[rsg], writes=[R_wcast[f % 2]])
                P.dma("sp", d_wst[f % 2], wgus[f], wcast[f % 2][:].rearrange("p a b -> p (a b)"), reads=[R_wcast[f % 2]], writes=[R_wsc[f]])

            with ExitStack() as sa:
                NPT = 6
                PT = [sbuf(sa, "PT%d" % i, [128, 512], BF16) for i in range(NPT)]
                R_PT = [Res("PT%d" % i) for i in range(NPT)]
                pS = [psum(sa, "pS%d" % i, [128, 512], F32) for i in range(NPT)]
                R_pS = [Res("pS%d" % i) for i in range(NPT)]
                pO = [psum(sa, "pO%d" % i, [128, 512], F32) for i in range(2)]
                R_pO = [Res("pO%d" % i) for i in range(2)]
                osb = [sbuf(sa, "osb%d" % i, [128, 512], F32) for i in range(2)]
                R_osb = [Res("osb%d" % i) for i in range(2)]
                rsum = [sbuf(sa, "rsum%d" % i, [128, 512], F32) for i in range(2)]
                R_rsum = [Res("rsum%d" % i) for i in range(2)]
                steps = [(qc, pr, kt) for qc in range(4) for pr in range(4) for kt in range(NKT)]
                LA = 2

                def emit_S(i):
                    qc, pr, kt = steps[i]
                    for half in range(2):
                        hd = 2 * pr + half
                        kvh = hd // 4
                        lo, hi = half * 64, half * 64 + 64
                        sb_ = (2 * i + half) % NPT
                        qsl = qT[lo:hi, pr, qc * 512:(qc + 1) * 512]
                        P.op("pe", lambda e, sb_=sb_, lo=lo, hi=hi, kvh=kvh, qsl=qsl: e.matmul(pS[sb_][:], lhsT=kTd[lo:hi, kvh, kt * 128:(kt + 1) * 128], rhs=qsl, start=True, stop=True),
                             reads=[R_k[kt], R_q[hd][qc]], writes=[R_pS[sb_]])
                    for half in range(2):
                        sb_ = (2 * i + half) % NPT
                        P.op("act", lambda e, sb_=sb_: e.activation(out=PT[sb_][:], in_=pS[sb_][:], func=AF.Exp, scale=0.125), reads=[R_pS[sb_]], writes=[R_PT[sb_]])

                def emit_PV(i):
                    qc, pr, kt = steps[i]
                    for half in range(2):
                        hd = 2 * pr + half
                        kvh = hd // 4
                        sb_ = (2 * i + half) % NPT
                        ob = half
                        if half == 0:
                            P.op("pe", lambda e, sb_=sb_, kvh=kvh: e.matmul(pO[0][:, :], lhsT=Vflat[:, (kt * 2 + kvh) * 130 + 64:(kt * 2 + kvh) * 130 + 192], rhs=PT[sb_][:], start=(kt == 0), stop=(kt == NKT - 1)),
                                 reads=[R_v[kt], R_PT[sb_]], writes=[R_pO[0]])
                        else:
                            P.op("pe", lambda e, sb_=sb_, kvh=kvh: e.matmul(pO[1][:, :], lhsT=Vaug[:, kt, kvh, 0:128], rhs=PT[sb_][:], start=(kt == 0), stop=(kt == NKT - 1)),
                                 reads=[R_v[kt], R_PT[sb_]], writes=[R_pO[1]])
                    if kt != NKT - 1:
                        return
                    z_pB, R_pB = pS[(2 * i) % NPT], R_pS[(2 * i) % NPT]
                    for half in range(2):
                        hd = 2 * pr + half
                        ob = half
                        lo, hi = half * 64, half * 64 + 64
                        qsl = qT[lo:hi, pr, qc * 512:(qc + 1) * 512]
                        srow = 64 if half == 0 else 0
                        nrow = 65 if half == 0 else 128
                        P.op("dve", lambda e, ob=ob, nrow=nrow: e.tensor_copy(out=osb[ob][0:nrow, :], in_=pO[ob][0:nrow, :]), reads=[R_pO[ob]], writes=[R_osb[ob]])
                        P.op("dve", lambda e, ob=ob, srow=srow: e.reciprocal(out=rsum[ob][srow:srow + 1, :], in_=osb[ob][srow:srow + 1, :]), reads=[R_osb[ob]], writes=[R_rsum[ob]])
                        P.op("pe", lambda e, ob=ob, srow=srow, z_pB=z_pB: e.matmul(z_pB[:], lhsT=ones32[srow:srow + 1, 0:128], rhs=rsum[ob][srow:srow + 1, :], start=True, stop=True),
                             reads=[R_rsum[ob], R_cst], writes=[R_pB])
                        P.op("dve", lambda e, ob=ob, lo=lo, hi=hi, qsl=qsl, z_pB=z_pB: e.tensor_tensor(out=qsl, in0=osb[ob][lo:hi, :], in1=z_pB[lo:hi, :], op=ALU.mult),
                             reads=[R_osb[ob], R_pB], writes=[R_q[hd][qc]])

                for i in range(len(steps) + LA):
                    if i < len(steps):
                        emit_S(i)
                    if i >= LA:
                        emit_PV(i - LA)
                dbg("attT", qT[:].rearrange("p a b -> p (a b)"), [128, 4 * 2048], [r for rr in R_q for r in rr], BF16)
                P.barrier()

            if DBG["stop"] == "C":
                P.barrier()
                P.emit()
                return nc

            with ExitStack() as sf:
                aoff[0] = 0
                actT = sbuf(sf, "actT", [128, NFT, 512], BF16)
                h2T = sbuf(sf, "h2T", [128, 8, 512], BF16)
                assert aoff[0] <= mark1
                aoff[0] = mark3
                z_gate_bc = sbuf(sf, "gate_bc4", [128, 2, 1024], F32)
                gfin_sb = sbuf(sf, "gfin_sb", [128, 1024], F32)
                z_R_gate = Res("gate4")
                P.dma("sp", P.dma_sem(), z_gate_bc[:].rearrange("p a b -> p (a b)"), gscr, reads=[R_gscr], writes=[z_R_gate])
                P.dma("sp", P.dma_sem(), gfin_sb[:], gfin, writes=[R_gfin])
                z_NXB = 1
                z_xt = [sbuf(sf, "fxt%d" % i, [128, 1024], F32) for i in range(z_NXB)]
                z_R_xt = [Res("fxt%d" % i) for i in range(z_NXB)]
                z_d_xt = [P.dma_sem() for _ in range(z_NXB)]
                x1 = sbuf(sf, "x1", [128, 4, 1024], F32)
                R_x1 = [Res("x1_%d" % i) for i in range(4)]
                z_junk = sbuf(sf, "fjunk", [128, 1024], BF16)
                z_R_junk = Res("fjunk")
                z_xs = [sbuf(sf, "fxs%d" % i, [128, 1024], BF16) for i in range(z_NXB)]
                z_R_xs = [Res("fxs%d" % i) for i in range(z_NXB)]
                z_stat = [sbuf(sf, "fstat%d" % i, [128, 4], F32) for i in range(z_NXB)]
                z_R_stat = [Res("fstat%d" % i) for i in range(z_NXB)]
                R_h2 = Res("h2T")
                R_act = [Res("act%d" % i) for i in range(NFT)]
                NWB = 3
                wgu_bf = [sbuf(sf, "wgu%d" % i, [128, 8, 256], BF16) for i in range(NWB)]
                R_wgu = [Res("wgu%d" % i) for i in range(NWB)]
                d_wld = [P.dma_sem() for _ in range(NWB)]
                fe = [sbuf(sf, "fe0", [128, 512], F32)] * 2
                R_fe = [Res("fe0")] * 2
                yo = [sbuf(sf, "yo%d" % i, [128, 1024], F32) for i in range(2)]
                R_yo = [Res("yo%d" % i) for i in range(2)]
                d_yo = [P.dma_sem() for _ in range(2)]
                pY = [psum(sf, "pY%d" % i, [128, 512], F32) for i in range(2)]
                R_pY = [Res("pY0"), Res("pY1")]
                z_pT = psum(sf, "fpT", [128, 1024], BF16)
                z_R_pT = Res("fpT")
                pA = [psum(sf, "pA%d" % i, [128, 512], F32) for i in range(2)]
                pBb = [psum(sf, "pBb%d" % i, [128, 512], F32) for i in range(2)]
                R_pA = [Res("pA0"), Res("pA1")]
                R_pBb = [Res("pBb0"), Res("pBb1")]
                xcnt = 0
                wcnt = 0
                ycnt = 0
                def wout_part(ci, j):
                    tile_own = ci * 4 + j
                    xb = 0
                    r0 = R_OWN + tile_own * 128
                    P.dma("sp", z_d_xt[xb], z_xt[xb][:], xin[r0:r0 + 128, :], writes=[z_R_xt[xb]])
                    for half in range(2):
                        for kc in range(8):
                            if kc < 4:
                                lhs = qT[:, kc, tile_own * 128:(tile_own + 1) * 128]
                                rl = [R_q[2 * kc][ci], R_q[2 * kc + 1][ci]]
                            else:
                                lhs = oA[:, kc - 4, tile_own * 128:(tile_own + 1) * 128]
                                rl = [R_oA[kc - 4][tile_own]]
                            P.op("pe", lambda e, lhs=lhs, kc=kc, half=half: e.matmul(pY[half][:], lhsT=lhs, rhs=wout_bf[:, kc, half * 512:(half + 1) * 512], start=(kc == 0), stop=(kc == 7)),
                                 reads=rl + [R_wout[half * 2], R_wout[half * 2 + 1]], writes=[R_pY[half]])
                        P.op("dve", lambda e, half=half, j=j: e.tensor_tensor(out=x1[:, j, half * 512:(half + 1) * 512], in0=pY[half][:], in1=z_gate_bc[:, 0, half * 512:(half + 1) * 512], op=ALU.mult),
                             reads=[R_pY[half], z_R_gate], writes=[R_x1[j]])
                    P.op("dve", lambda e, j=j, xb=xb: e.tensor_tensor(out=x1[:, j, :], in0=x1[:, j, :], in1=z_xt[xb][:], op=ALU.add), reads=[R_x1[j], z_R_xt[xb]], writes=[R_x1[j]])

                def norm_a(ci, j):
                    xb = 0
                    src = x1[:, j, :]
                    P.op("dve", lambda e, src=src, xb=xb: e.scalar_tensor_tensor(out=z_junk[:], in0=src, scalar=1.0, in1=src, op0=ALU.mult, op1=ALU.mult, accum_out=z_stat[xb][:, 0:1]),
                         reads=[R_x1[j]], writes=[z_R_junk, z_R_stat[xb]])
                    P.op("act", lambda e, xb=xb: e.activation(out=z_stat[xb][:, 1:2], in_=z_stat[xb][:, 0:1], func=AF.Ln, bias=EPS, scale=1.0 / D), reads=[z_R_stat[xb]], writes=[z_R_stat[xb]])
                    P.op("act", lambda e, xb=xb: e.activation(out=z_stat[xb][:, 2:3], in_=z_stat[xb][:, 1:2], func=AF.Exp, scale=-0.5), reads=[z_R_stat[xb]], writes=[z_R_stat[xb]])
                    P.op("act", lambda e, src=src, xb=xb: e.activation(out=z_xs[xb][:], in_=src, func=AF.Copy, scale=z_stat[xb][:, 2:3]), reads=[R_x1[j], z_R_stat[xb]], writes=[z_R_xs[xb]])

                def norm_b(ci, j):
                    xb = 0
                    for kc in range(8):
                        P.op("pe", lambda e, kc=kc, xb=xb: e.transpose(out=z_pT[:, kc * 128:(kc + 1) * 128], in_=z_xs[xb][:, kc * 128:(kc + 1) * 128], identity=ident),
                             reads=[z_R_xs[xb], R_cst], writes=[z_R_pT])
                    for kc in range(8):
                        o = h2T[:, kc, j * 128:(j + 1) * 128]
                        i_ = z_pT[:, kc * 128:(kc + 1) * 128]
                        if j % 2 == 0:
                            P.op("act", lambda e, o=o, i_=i_, kc=kc: e.activation(out=o, in_=i_, func=AF.Identity, bias=modv[:, 5, kc:kc + 1], scale=modv[:, 4, kc:kc + 1]),
                                 reads=[z_R_pT, R_modv], writes=[R_h2])
                        else:
                            P.op("dve", lambda e, o=o, i_=i_, kc=kc: e.tensor_scalar(out=o, in0=i_, scalar1=modv[:, 4, kc:kc + 1], scalar2=modv[:, 5, kc:kc + 1], op0=ALU.mult, op1=ALU.add),
                                 reads=[z_R_pT, R_modv], writes=[R_h2])


                for ci in range(4):
                    wout_part(ci, 0)
                    for j in range(4):
                        norm_a(ci, j)
                        if j < 3:
                            wout_part(ci, j + 1)
                        norm_b(ci, j)
                    if ci == 0:
                        dbg("x1", x1[:].rearrange("p a b -> p (a b)"), [128, 4096], R_x1)
                        dbg("h2T", h2T[:].rearrange("p a b -> p (a b)"), [128, 4096], [R_h2], BF16)
                    for f in range(NFT):
                        wb = wcnt % NWB
                        pb = wcnt % 2
                        wcnt += 1
                        P.dma("sp", d_wld[wb], wgu_bf[wb][:].rearrange("p a b -> p (a b)"), wgus[f], reads=[R_wsc[f]], writes=[R_wgu[wb]])
                        for kc in range(8):
                            P.op("pe", lambda e, kc=kc, wb=wb, pb=pb: e.matmul(pA[pb][:], lhsT=wgu_bf[wb][:, kc, 0:128], rhs=h2T[:, kc, :], start=(kc == 0), stop=(kc == 7)),
                                 reads=[R_wgu[wb], R_h2], writes=[R_pA[pb]])
                        for kc in range(8):
                            P.op("pe", lambda e, kc=kc, wb=wb, pb=pb: e.matmul(pBb[pb][:], lhsT=wgu_bf[wb][:, kc, 128:256], rhs=h2T[:, kc, :], start=(kc == 0), stop=(kc == 7)),
                                 reads=[R_wgu[wb], R_h2], writes=[R_pBb[pb]])
                        P.op("act", lambda e, pb=pb: e.activation(out=fe[pb][:], in_=pA[pb][:], func=AF.Silu), reads=[R_pA[pb]], writes=[R_fe[pb]])
                        P.op("dve", lambda e, pb=pb, f=f: e.tensor_tensor(out=actT[:, f, :], in0=pBb[pb][:], in1=fe[pb][:], op=ALU.mult), reads=[R_pBb[pb], R_fe[pb]], writes=[R_act[f]])
                    if ci == 0:
                        dbg("actT", actT[:].rearrange("p a b -> p (a b)"), [128, NFT * 512], R_act, BF16)
                    for j in range(4):
                        tile_own = ci * 4 + j
                        yb = ycnt % 2
                        ycnt += 1
                        for half in range(2):
                            for f in range(NFT):
                                P.op("pe", lambda e, f=f, half=half, j=j: e.matmul(pY[half][:], lhsT=actT[:, f, j * 128:(j + 1) * 128], rhs=wdown_bf[:, f, half * 512:(half + 1) * 512], start=(f == 0), stop=(f == NFT - 1)),
                                     reads=[R_act[f], R_wdown[f // 2]], writes=[R_pY[half]])
                            P.op("dve", lambda e, half=half, yb=yb: e.tensor_tensor(out=yo[yb][:, half * 512:(half + 1) * 512], in0=pY[half][:], in1=z_gate_bc[:, 1, half * 512:(half + 1) * 512], op=ALU.mult),
                                 reads=[R_pY[half], z_R_gate], writes=[R_yo[yb]])
                        P.op("dve", lambda e, j=j, yb=yb: e.tensor_tensor(out=yo[yb][:], in0=yo[yb][:], in1=x1[:, j, :], op=ALU.add), reads=[R_yo[yb], R_x1[j]], writes=[R_yo[yb]])
                        sb2 = 0
                        P.op("dve", lambda e, yb=yb, sb2=sb2: e.scalar_tensor_tensor(out=z_junk[:], in0=yo[yb][:], scalar=1.0, in1=yo[yb][:], op0=ALU.mult, op1=ALU.mult, accum_out=z_stat[sb2][:, 0:1]),
                             reads=[R_yo[yb]], writes=[z_R_junk, z_R_stat[sb2]])
                        P.op("act", lambda e, sb2=sb2: e.activation(out=z_stat[sb2][:, 1:2], in_=z_stat[sb2][:, 0:1], func=AF.Ln, bias=EPS, scale=1.0 / D), reads=[z_R_stat[sb2]], writes=[z_R_stat[sb2]])
                        P.op("act", lambda e, sb2=sb2: e.activation(out=z_stat[sb2][:, 2:3], in_=z_stat[sb2][:, 1:2], func=AF.Exp, scale=-0.5), reads=[z_R_stat[sb2]], writes=[z_R_stat[sb2]])
                        P.op("dve", lambda e, yb=yb, sb2=sb2: e.scalar_tensor_tensor(out=yo[yb][:], in0=yo[yb][:], scalar=z_stat[sb2][:, 2:3], in1=gfin_sb[:], op0=ALU.mult, op1=ALU.mult),
                             reads=[R_yo[yb], z_R_stat[sb2], R_gfin], writes=[R_yo[yb]])
                        P.dma("sp", d_yo[yb], y[tile_own * 128:(tile_own + 1) * 128, :], yo[yb][:], reads=[R_yo[yb]])
                P.barrier()
        P.barrier()
        P.emit()
    return nc


def _rope_tables():
    n_rows = 4096 // 64
    row = np.repeat(np.arange(n_rows, dtype=np.float32), 64)
    col = np.tile(np.arange(64, dtype=np.float32), n_rows)
    inv = (10000.0 ** (-np.arange(0, 32, 2, dtype=np.float32) / 32)).astype(np.float32)
    ar = row[:, None] * inv
    ac = col[:, None] * inv
    ang = np.concatenate([ar, ar, ac, ac], axis=-1)
    return np.cos(ang).astype(np.float32), np.sin(ang).astype(np.float32)


def _consts():
    ident = np.eye(128, dtype=np.float32)
    blk = np.zeros((128, 128), np.float32)
    blk[:64, :64] = 1.0 / 64
    blk[64:, 64:] = 1.0 / 64
    perm = np.zeros((128, 128), np.float32)
    for m in range(128):
        if (m % 32) < 16:
            perm[m + 16, m] = -1.0
        else:
            perm[m - 16, m] = 1.0
    ones = np.full((128, 128), 1.0 / 128, np.float32)
    s = np.arange(128)[:, None]
    t = np.arange(128)[None, :]
    maskf = (s <= t).astype(np.float32)
    maskb = (s >= t).astype(np.float32)
    return np.concatenate([ident, blk, perm, ones, maskf, maskb], axis=1)


def _kc(w):
    K, N = w.shape
    return np.ascontiguousarray(w.reshape(K // 128, 128, N).transpose(1, 0, 2))


def prepare_inputs(x, c, ctx, c_ctx, w_mod, b_mod, g_norm1, w_in, g_q, g_k, lb_raw, g_hg, w_out,
                   g_norm2, w_gu, w_down, g_final):
    f = lambda a: np.asarray(a, dtype=np.float32)
    x, c, ctx, c_ctx = f(x), f(c), f(ctx), f(c_ctx)
    w_mod, b_mod, w_in, w_out, w_gu, w_down = f(w_mod)[0], f(b_mod)[0], f(w_in)[0], f(w_out)[0], f(w_gu)[0], f(w_down)[0]
    g1, g2, gq, gk, ghg, gfin = f(g_norm1)[0], f(g_norm2)[0], f(g_q)[0], f(g_k)[0], f(g_hg)[0], f(g_final)
    lb_raw = f(lb_raw)
    cos, sin = _rope_tables()
    cst = _consts()
    wmod_l = _kc(w_mod)
    bsel = np.concatenate([b_mod[s * 1024:(s + 1) * 1024] for s in (0, 1, 3, 4)])
    bmodT = np.repeat(bsel.reshape(32, 128).T[:, :, None], 2, axis=2).reshape(128, 64)
    bmodrep = np.ascontiguousarray(np.broadcast_to(np.concatenate([b_mod[2048:3072], b_mod[5120:6144]])[None, :], (128, 2048)))
    gn = np.concatenate([g1.reshape(8, 128).T, g2.reshape(8, 128).T], axis=1)
    gfin_rep = np.ascontiguousarray(np.broadcast_to(gfin[None, :], (128, 1024)))
    gsm = np.stack([np.tile(gq, 2), np.tile(gk, 2), ghg], axis=1)
    wout_l = _kc(w_out)
    wgu_l = np.stack([np.concatenate([_kc(w_gu[:, j * 128:(j + 1) * 128]), _kc(w_gu[:, DFF + j * 128: DFF + (j + 1) * 128])], axis=2) for j in range(NFT)])
    wdown_l = _kc(w_down)
    maps = []
    for b in range(4):
        for h in range(2):
            if h == 1:
                own = np.arange(2048, 4096)
                oth = np.arange(0, 2048)
                cidx = np.arange(256)
                fA, fB, dA, dB = slice(1792, 2304), slice(2304, 2816), 0, 1
            else:
                own = np.arange(2047, -1, -1)
                oth = np.arange(4095, 2047, -1)
                cidx = np.arange(255, -1, -1)
                fA, fB, dA, dB = slice(2304, 2816), slice(1792, 2304), 1, 0
            xin = np.concatenate([ctx[b][cidx], x[b][oth], x[b][own]], axis=0)
            tl = np.concatenate([oth, own])
            ropec = np.ascontiguousarray(np.tile(cos[tl].T, (2, 1)))
            ropes = np.ascontiguousarray(np.tile(sin[tl].T, (2, 1)))
            cT2 = np.stack([c[b].reshape(8, 128).T, c_ctx.reshape(8, 128).T], axis=2).reshape(128, 16)
            cTrep = np.repeat(c[b].reshape(8, 128).T[:, :, None], 128, axis=2).reshape(128, 1024)
            k0, k1 = w_in[:, 512:576], w_in[:, 576:640]
            win_l = np.concatenate([w_in[:, 0:512], k0, k0, k1, k1, w_in[:, 640:768], w_in[:, 768:1280], w_in[:, 1280:1792],
                                    w_in[:, fA], w_in[:, fB], w_in[:, 2816:3328]], axis=1)
            lbl = np.stack([lb_raw[:, dA, :], lb_raw[:, dB, :]], axis=1)
            lbl = lbl.reshape(2, 2, 4, 128).transpose(3, 0, 1, 2).reshape(128, 16)
            maps.append({
                "xin": np.ascontiguousarray(xin), "ropec": ropec, "ropes": ropes,
                "cT2": np.ascontiguousarray(cT2), "cTrep": np.ascontiguousarray(cTrep),
                "wmod": wmod_l, "bmodT": np.ascontiguousarray(bmodT), "bmodrep": bmodrep,
                "gn": np.ascontiguousarray(gn), "gfin": gfin_rep, "gsm": np.ascontiguousarray(gsm),
                "lbraw": np.ascontiguousarray(lbl), "win": _kc(win_l), "wout": wout_l, "wgu": wgu_l,
                "wdown": wdown_l, "cst": cst,
            })
    return maps


def assemble(results):
    out = np.empty((4, 4096, 1024), np.float32)
    i = 0
    for b in range(4):
        for h in range(2):
            yl = np.asarray(results[i]["y"], dtype=np.float32)
            if h == 1:
                out[b, 2048:4096] = yl
            else:
                out[b, 0:2048] = yl[::-1]
            i += 1
    return out


def kernel(**inputs):
    maps = prepare_inputs(**inputs)
    nc = build()
    res = run_bass_kernel_spmd(nc, maps, core_ids=list(range(8)))
    return assemble(res.results)
```
